# Optimizing a Trainium2 kernel written in Bass

```python
import math
import jax, jax.numpy as jnp
from jax import lax
import numpy as np

D_MODEL = 1024
BATCH = 2
SEQ = 8192
DEPTH = 1

MIX_WIDTH = D_MODEL
ATTN_WIDTH = MIX_WIDTH // 2
SSM_WIDTH = MIX_WIDTH - ATTN_WIDTH
HEAD_DIM = 64
N_Q_HEADS = ATTN_WIDTH // HEAD_DIM
N_KV_HEADS = 2
KV_GROUP = N_Q_HEADS // N_KV_HEADS
WINDOW = 128
BLOCK = 128
ROPE_THETA = 500000.0
ROT_DIM = HEAD_DIM // 4
SSM_GROUP_CH = 16
N_SSM_GROUPS = SSM_WIDTH // SSM_GROUP_CH
SSM_STATE = 64
DT_MIN = 0.001
DT_MAX = 0.1
D_FF = 128 * ((8 * D_MODEL // 3 + 127) // 128)
CONV_WIDTH = 3
NORM_EPS = 1e-5
Q_COLS = N_Q_HEADS * HEAD_DIM
KV_COLS = N_KV_HEADS * HEAD_DIM
IN_COLS = Q_COLS + 2 * KV_COLS + SSM_WIDTH
MASK_VALUE = -1e30

kernel_name = "hybrid_swa_sink_s5_convffn"


def rms_norm(x, g):
    xf = x.astype(jnp.float32)
    y = xf * lax.rsqrt(jnp.mean(xf * xf, axis=-1, keepdims=True) + NORM_EPS)
    return (y * g.astype(jnp.float32)).astype(x.dtype)


def partial_rotary(x, pos):
    half = ROT_DIM // 2
    inv_freq = ROPE_THETA ** (-jnp.arange(half, dtype=jnp.float32) * 2.0 / ROT_DIM)
    ang = pos[:, None] * inv_freq[None, :]
    cos = jnp.cos(ang)[None, :, None, :]
    sin = jnp.sin(ang)[None, :, None, :]
    xf = x.astype(jnp.float32)
    x1 = xf[..., :half]
    x2 = xf[..., half:ROT_DIM]
    out = jnp.concatenate([x1 * cos - x2 * sin, x2 * cos + x1 * sin, xf[..., ROT_DIM:]], axis=-1)
    return out.astype(x.dtype)


def sliding_window_gqa(q, k, v, sinks):
    b, l = q.shape[0], q.shape[1]
    nb = l // BLOCK
    qb = q.astype(jnp.float32).reshape(b, nb, BLOCK, N_KV_HEADS, KV_GROUP, HEAD_DIM)
    kb = k.astype(jnp.float32).reshape(b, nb, BLOCK, N_KV_HEADS, HEAD_DIM)
    vb = v.astype(jnp.float32).reshape(b, nb, BLOCK, N_KV_HEADS, HEAD_DIM)
    shift = lambda t: jnp.pad(t, ((0, 0), (1, 0), (0, 0), (0, 0), (0, 0)))[:, :-1]
    kk = jnp.concatenate([shift(kb), kb], axis=2)
    vv = jnp.concatenate([shift(vb), vb], axis=2)
    s = jnp.einsum('bnqhgd,bnkhd->bnhgqk', qb, kk) * (HEAD_DIM ** -0.5)
    qi = jnp.arange(BLOCK)[:, None]
    kj = jnp.arange(2 * BLOCK)[None, :]
    diff = qi + BLOCK - kj
    band = (diff >= 0) & (diff < WINDOW)
    has_prev = (jnp.arange(nb)[:, None] > 0) | (kj >= BLOCK)
    mask = band[None, :, :] & has_prev[:, None, :]
    s = jnp.where(mask[None, :, None, None, :, :], s, MASK_VALUE)
    sink = sinks.astype(jnp.float32).reshape(N_KV_HEADS, KV_GROUP)[None, None, :, :, None, None]
    m = jnp.maximum(jnp.max(s, axis=-1, keepdims=True), sink)
    p = jnp.exp(s - m)
    denom = jnp.sum(p, axis=-1, keepdims=True) + jnp.exp(sink - m)
    o = jnp.einsum('bnhgqk,bnkhd->bnqhgd', p / denom, vv)
    return o.reshape(b, l, N_Q_HEADS * HEAD_DIM).astype(q.dtype)


def s5_ssm(u, lam_re, lam_im, log_step, b_re, b_im, c_re, c_im, d_skip):
    b, l = u.shape[0], u.shape[1]
    uf = u.astype(jnp.float32).reshape(b, l, N_SSM_GROUPS, SSM_GROUP_CH)
    lam = lax.complex(lam_re.astype(jnp.float32), lam_im.astype(jnp.float32))
    dt = jnp.exp(log_step.astype(jnp.float32))[:, None]
    lam_bar = jnp.exp(lam * dt)
    b_c = lax.complex(b_re.astype(jnp.float32), b_im.astype(jnp.float32))
    c_c = lax.complex(c_re.astype(jnp.float32), c_im.astype(jnp.float32))
    b_bar = ((lam_bar - 1.0) / lam)[..., None] * b_c
    bu = jnp.einsum('gpc,blgc->blgp', b_bar, uf.astype(jnp.complex64))
    a = jnp.broadcast_to(lam_bar, (1, l, N_SSM_GROUPS, SSM_STATE))

    def combine(e_i, e_j):
        a_i, s_i = e_i
        a_j, s_j = e_j
        return a_j * a_i, a_j * s_i + s_j

    _, states = lax.associative_scan(combine, (a, bu), axis=1)
    y = jnp.einsum('gcp,blgp->blgc', c_c, states).real + d_skip.astype(jnp.float32) * uf
    return y.reshape(b, l, SSM_WIDTH).astype(u.dtype)


def conv_ffn(h, w_up, conv_w, conv_b, w_down):
    up = h @ w_up
    up = lax.conv_general_dilated(
        up, conv_w[:, None, :], window_strides=(1,), padding=[(CONV_WIDTH - 1, 0)],
        dimension_numbers=('NWC', 'WIO', 'NWC'), feature_group_count=2 * D_FF) + conv_b
    gate, val = up[..., :D_FF], up[..., D_FF:]
    return (jax.nn.silu(gate) * val) @ w_down


def setup_inputs(seed: int = 0) -> dict:
    key = jax.random.key(seed)
    ks = jax.random.split(key, 28)
    f32 = jnp.float32
    nrm = lambda k, shape, s: jax.random.normal(k, shape, f32) * s
    gain = lambda k, shape: 1.0 + 0.01 * jax.random.normal(k, shape, f32)
    L_, G, P, C = DEPTH, N_SSM_GROUPS, SSM_STATE, SSM_GROUP_CH
    x = jax.random.normal(ks[0], (BATCH, SEQ, D_MODEL), f32)
    lam_im = math.pi * jnp.broadcast_to(jnp.arange(P, dtype=f32), (L_, G, P)) + 0.01 * jax.random.normal(ks[6], (L_, G, P), f32)
    log_step = jax.random.uniform(ks[7], (L_, G), f32, math.log(DT_MIN), math.log(DT_MAX))
    return {
        "x": x,
        "ln1_g": gain(ks[1], (L_, D_MODEL)),
        "w_in": nrm(ks[2], (L_, D_MODEL, IN_COLS), D_MODEL ** -0.5),
        "b_in": nrm(ks[3], (L_, IN_COLS), 0.02),
        "sinks": nrm(ks[4], (L_, N_Q_HEADS), 1.0),
        "lam_re": -0.5 + 0.01 * jax.random.normal(ks[5], (L_, G, P), f32),
        "lam_im": lam_im,
        "log_step": log_step,
        "ssm_b_re": nrm(ks[8], (L_, G, P, C), (2 * C) ** -0.5),
        "ssm_b_im": nrm(ks[9], (L_, G, P, C), (2 * C) ** -0.5),
        "ssm_c_re": nrm(ks[10], (L_, G, C, P), (2 * P) ** -0.5),
        "ssm_c_im": nrm(ks[11], (L_, G, C, P), (2 * P) ** -0.5),
        "ssm_d": nrm(ks[12], (L_, G, C), 1.0),
        "w_glu": nrm(ks[13], (L_, SSM_WIDTH, SSM_WIDTH), SSM_WIDTH ** -0.5),
        "b_glu": nrm(ks[14], (L_, SSM_WIDTH), 0.02),
        "g_attn": gain(ks[15], (L_, ATTN_WIDTH)),
        "g_ssm": gain(ks[16], (L_, SSM_WIDTH)),
        "w_out": nrm(ks[17], (L_, MIX_WIDTH, D_MODEL), MIX_WIDTH ** -0.5),
        "ln2_g": gain(ks[18], (L_, D_MODEL)),
        "w_up": nrm(ks[19], (L_, D_MODEL, 2 * D_FF), D_MODEL ** -0.5),
        "conv_w": nrm(ks[20], (L_, CONV_WIDTH, 2 * D_FF), CONV_WIDTH ** -0.5),
        "conv_b": nrm(ks[21], (L_, 2 * D_FF), 0.02),
        "w_down": nrm(ks[22], (L_, D_FF, D_MODEL), D_FF ** -0.5),
        "lnf_g": gain(ks[23], (D_MODEL,)),
    }


def reference(x, ln1_g, w_in, b_in, sinks, lam_re, lam_im, log_step, ssm_b_re, ssm_b_im,
              ssm_c_re, ssm_c_im, ssm_d, w_glu, b_glu, g_attn, g_ssm, w_out, ln2_g,
              w_up, conv_w, conv_b, w_down, lnf_g):
    b, l = x.shape[0], x.shape[1]
    pos = jnp.arange(l, dtype=jnp.float32)
    h = x
    for i in range(DEPTH):
        hn = rms_norm(h, ln1_g[i])
        proj = hn @ w_in[i] + b_in[i]
        q = proj[..., :Q_COLS].reshape(b, l, N_Q_HEADS, HEAD_DIM)
        k = proj[..., Q_COLS:Q_COLS + KV_COLS].reshape(b, l, N_KV_HEADS, HEAD_DIM)
        v = proj[..., Q_COLS + KV_COLS:Q_COLS + 2 * KV_COLS].reshape(b, l, N_KV_HEADS, HEAD_DIM)
        u = proj[..., Q_COLS + 2 * KV_COLS:]
        q = partial_rotary(q, pos)
        k = partial_rotary(k, pos)
        attn = sliding_window_gqa(q, k, v, sinks[i])
        y = jax.nn.gelu(s5_ssm(u, lam_re[i], lam_im[i], log_step[i], ssm_b_re[i], ssm_b_im[i],
                               ssm_c_re[i], ssm_c_im[i], ssm_d[i]))
        ssm = y * jax.nn.sigmoid(y @ w_glu[i] + b_glu[i])
        mix = jnp.concatenate([rms_norm(attn, g_attn[i]), rms_norm(ssm, g_ssm[i])], axis=-1)
        h = h + mix @ w_out[i]
        h = h + conv_ffn(rms_norm(h, ln2_g[i]), w_up[i], conv_w[i], conv_b[i], w_down[i])
    return rms_norm(h, lnf_g)
```

```python
import math
import os
from contextlib import ExitStack

import numpy as np
import ml_dtypes

import concourse.bass as bass
import concourse.mybir as mybir
from concourse.bass_utils import run_bass_kernel_spmd

F32 = mybir.dt.float32
BF = mybir.dt.bfloat16
AF = mybir.ActivationFunctionType
ALU = mybir.AluOpType
AX = mybir.AxisListType

D = 1024
SEQ = 8192
QT = 2048
NH = 8
G = 32
DFF = 2816
NFC = DFF // 128
EPS = 1e-5
PI = math.pi
TWO_PI = 2.0 * math.pi
NW = 53000
NTOK = QT + 128
NPT = 2560

ENGS = ["pe", "act", "dve", "pool", "sp"]


class Sch:
    def __init__(self):
        self.q = {e: [] for e in ENGS}
        self.cnt = {e: 0 for e in ENGS}
        self.seen = {e: {} for e in ENGS}
        self.lw = {}
        self.rd = {}
        self.dcnt = {}

    def _deps(self, eng, reads, writes):
        need = {}

        def add(ev):
            h, v = ev
            if need.get(h, 0) < v:
                need[h] = v

        for b in list(reads) + list(writes):
            if b in self.lw:
                add(self.lw[b])
        for b in writes:
            for h, v in self.rd.get(b, {}).items():
                add((h, v))
        out = []
        for h, v in need.items():
            if self.seen[eng].get(h, 0) < v:
                self.seen[eng][h] = v
                out.append((h, v))
        return out

    def _commit(self, ev, reads, writes):
        for b in writes:
            self.lw[b] = ev
            self.rd[b] = {}
        for b in reads:
            d = self.rd.setdefault(b, {})
            if d.get(ev[0], 0) < ev[1]:
                d[ev[0]] = ev[1]

    def op(self, eng, fn, reads=(), writes=()):
        waits = self._deps(eng, reads, writes)
        self.cnt[eng] += 1
        ev = (eng, self.cnt[eng])
        self.q[eng].append((waits, fn, eng))
        self._commit(ev, reads, writes)

    def dma(self, eng, out_ap, in_ap, reads=(), writes=(), key=None):
        waits = self._deps(eng, reads, writes)
        k = ("d", key or writes[0])
        self.dcnt[k] = self.dcnt.get(k, 0) + 16
        ev = (k, self.dcnt[k])
        self.q[eng].append((waits, lambda e, o=out_ap, i=in_ap: e.dma_start(out=o, in_=i), k))
        self._commit(ev, reads, writes)

    def fence(self):
        for a in ENGS:
            waits = []
            for e in ENGS:
                if e != a and self.cnt[e] > self.seen[a].get(e, 0):
                    self.seen[a][e] = self.cnt[e]
                    waits.append((e, self.cnt[e]))
            for k, v in self.dcnt.items():
                if v > self.seen[a].get(k, 0):
                    self.seen[a][k] = v
                    waits.append((k, v))
            self.q[a].append((waits, None, None))


class Arena:
    def __init__(self, t):
        self.t = t
        self.top = 0

    def f32(self, n):
        a = self.t[:, self.top:self.top + n]
        self.top += n
        assert self.top <= NW, self.top
        return a

    def bf(self, n):
        w = (n + 1) // 2
        a = self.t[:, self.top:self.top + w].bitcast(BF)
        self.top += w
        assert self.top <= NW, self.top
        return a[:, 0:n]


class _Stop(Exception):
    pass


def build_nc(stop=None, dump=(0, 16384)):
    nc = bass.Bass("TRN2", target_bir_lowering=False)
    S = Sch()

    def checkpoint(name):
        if stop == name:
            S.fence()
            ov = out_d.rearrange("(p a) n -> p (a n)", p=128)
            n = dump[1] - dump[0]
            S.dma("sp", ov[:, 0:n], arena_t[:, dump[0]:dump[1]], writes=["dump"], key="dump")
            S.fence()
            raise _Stop()

    def din(name, shape, dt=F32):
        return nc.dram_tensor(name, list(shape), dt, kind="ExternalInput").ap()

    xw = din("xw", [4, QT, D])
    win = din("win", [15, 128, 8 * 128])
    ball_d = din("ball", [128, 15])
    g1_d = din("g1", [128, 8])
    bvb_d = din("bvb", [128, 128])
    ropec_d = din("ropec", [128, NPT])
    ropes_d = din("ropes", [128, NPT])
    mask_d = din("mask", [128, 512])
    sink_d = din("sinkb", [128, 8])
    flags_d = din("flags", [128, 4])
    identb_d = din("identb", [128, 128], BF)
    identf_d = din("identf", [128, 128])
    swapj_d = din("swapj", [128, 128])
    ones_d = din("onesm", [128, 128])
    lre_d = din("lre", [128, G])
    lim_d = din("lim", [128, G])
    lstep_d = din("lstep", [128, G])
    sgn_d = din("sgn", [128, 1])
    bs1_d = din("bs1", [128, G * 16])
    bs2_d = din("bs2", [128, G * 16])
    csa_d = din("csa", [128, 4 * 128])
    csb_d = din("csb", [128, 4 * 128])
    dsk_d = din("dsk", [128, 4])
    tcol_d = din("tcol", [128, QT])
    wglu_d = din("wglu", [512, 512])
    bglu_d = din("bglu", [128, 4])
    wout_d = din("wout", [D, D])
    gmix_d = din("gmix", [128, 8])
    g2_d = din("g2", [128, 8])
    wup_d = din("wup", [NFC, 128, 8 * 256])
    cw_d = din("cw", [128, 44 * 3])
    cb_d = din("cb", [128, 44])
    wdown_d = din("wdown", [DFF, D])
    lnf_d = din("lnf", [128, D])
    out_d = nc.dram_tensor("out", [QT, D], F32, kind="ExternalOutput").ap()
    hs_d = nc.dram_tensor("hs", [NTOK, D], F32, kind="Internal").ap()

    es = ExitStack()
    arena_t = es.enter_context(nc.sbuf_tensor("arena", [128, NW], F32))
    AR = Arena(arena_t)
    ps = [es.enter_context(nc.psum_tensor(f"ps{i}", [128, 512], F32)) for i in range(8)]
    psk = [f"ps{i}" for i in range(8)]
    esem = {e: es.enter_context(nc.semaphore(f"sem_{e}")) for e in ENGS}

    ident_b = AR.bf(128)
    ident_f = AR.f32(128)
    swapj = AR.f32(128)
    onesm = AR.f32(128)
    flags = AR.f32(4)
    maskt = AR.f32(512)
    sinkb = AR.f32(8)
    ball = AR.f32(15)
    g1 = AR.f32(8)
    bvb = AR.f32(128)
    sgn = AR.f32(1)
    dsk = AR.f32(4)
    bglu = AR.f32(4)
    gmix = AR.f32(8)
    g2 = AR.f32(8)
    cw = AR.f32(132)
    cb = AR.f32(44)
    negpi = AR.f32(1)
    for dst, src in [(ident_b, identb_d), (ident_f, identf_d), (swapj, swapj_d), (onesm, ones_d),
                     (flags, flags_d), (maskt, mask_d), (sinkb, sink_d), (ball, ball_d), (g1, g1_d),
                     (bvb, bvb_d), (sgn, sgn_d), (dsk, dsk_d), (bglu, bglu_d), (gmix, gmix_d),
                     (g2, g2_d), (cw, cw_d), (cb, cb_d)]:
        S.dma("sp", dst, src, writes=["consts"], key="setup")
    halfpi = AR.f32(1)
    S.op("dve", lambda e: e.memset(negpi, -PI), writes=["negpi"])
    S.op("dve", lambda e: e.memset(halfpi, 0.5 * PI), writes=["negpi"])

    def rsqrt_ops(X, key):
        S.op("act", lambda e: e.activation(X, X, AF.Sqrt), reads=[key], writes=[key])
        S.op("dve", lambda e: e.reciprocal(X, X), reads=[key], writes=[key])

    I32 = mybir.dt.int32

    def trig_ops(phi, ki, ab, out_sin, out_cos, kphi, kki, kab, ksin, kcos, extra_reads=()):
        S.op("dve", lambda e: e.tensor_scalar(ki, phi, 1.0 / TWO_PI, None, ALU.mult),
             reads=[kphi] + list(extra_reads), writes=[kki])
        S.op("dve", lambda e: e.scalar_tensor_tensor(phi, ki, -TWO_PI, phi, ALU.mult, ALU.add),
             reads=[kki, kphi], writes=[kphi])
        S.op("dve", lambda e: e.tensor_scalar(phi, phi, PI, -PI, ALU.min, ALU.max), reads=[kphi], writes=[kphi])
        S.op("act", lambda e: e.activation(out_sin, phi, AF.Sin), reads=[kphi], writes=[ksin])
        S.op("act", lambda e: e.activation(ab, phi, AF.Abs), reads=[kphi, kki], writes=[kab])
        S.op("act", lambda e: e.activation(out_cos, ab, AF.Sin, bias=halfpi, scale=-1.0), reads=[kab, "negpi"], writes=[kcos])
    R_MIX = AR.top
    mixT = AR.bf(8 * NTOK).rearrange("p (c t) -> p c t", c=8)
    persist_top = AR.top

    try:
        winb = AR.bf(8 * 1920).rearrange("p (c n) -> p c n", c=8)
        uT = AR.bf(4 * 4 * QT).rearrange("p (c k t) -> p c k t", c=4, k=4)
        qTt = AR.bf(4 * NPT).rearrange("p (c t) -> p c t", c=4)
        kTt = AR.bf(NPT)
        Vt = AR.bf(20 * 128).rearrange("p (j n) -> p j n", j=20)
        phaseB_top = AR.top
        R_A = persist_top
        R_U = R_A + 8 * 1920 // 2
        R_Q = R_U + 4 * 4 * QT // 2
        wst = [AR.f32(1024).rearrange("p (c n) -> p c n", c=8) for _ in range(2)]
        AR.top = phaseB_top
        hnT = [AR.bf(8 * 512).rearrange("p (c t) -> p c t", c=8) for _ in range(2)]
        xt = [AR.f32(1024) for _ in range(3)]
        hnb = [AR.bf(1024) for _ in range(3)]
        sst = [AR.f32(2) for _ in range(3)]
        rct = [AR.f32(512)]
        rst = [AR.f32(512)]
        tA = AR.f32(512)
        tB = AR.f32(512)

        checkpoint("A0")
        for ch in range(15):
            sl = ch % 2
            S.dma("sp", wst[sl].rearrange("p c n -> p (c n)"), win[ch], writes=[f"wst{sl}"])
            S.op("pool" if ch % 2 == 0 else "dve", lambda e, sl=sl, ch=ch: e.tensor_tensor(
                winb[:, :, ch * 128:(ch + 1) * 128], wst[sl],
                g1.unsqueeze(2).to_broadcast([128, 8, 128]), ALU.mult),
                reads=[f"wst{sl}", "consts"], writes=[f"winb{ch}"])
        S.fence()
        checkpoint("A1")

        def rms_tile(src_dram, sl, sskey):
            S.dma("sp", xt[sl], src_dram, writes=[f"xt{sl}"])
            S.op("dve", lambda e, sl=sl: e.scalar_tensor_tensor(hnb[sl], xt[sl], 1.0, xt[sl], ALU.mult, ALU.mult,
                                                               accum_out=sst[sl][:, 0:1]),
                 reads=[f"xt{sl}"], writes=[f"hnb{sl}", f"ss{sl}"])
            S.op("dve", lambda e, sl=sl: e.tensor_scalar(sst[sl][:, 1:2], sst[sl][:, 0:1], 1.0 / D, EPS, ALU.mult, ALU.add),
                 reads=[f"ss{sl}"], writes=[f"rs{sl}"])
            rsqrt_ops(sst[sl][:, 1:2], f"rs{sl}")
            S.op("dve", lambda e, sl=sl: e.tensor_scalar(hnb[sl], xt[sl], sst[sl][:, 1:2], None, ALU.mult),
                 reads=[f"xt{sl}", f"rs{sl}"], writes=[f"hnb{sl}"])

        def transpose8(src_bf, pbank, dst_fn, rkeys, wkeys, nchunks=8):
            pv = ps[pbank][:].bitcast(BF)

            def f(e):
                ins = None
                for c in range(nchunks):
                    ins = e.transpose(pv[:, c * 128:(c + 1) * 128], src_bf[:, c * 128:(c + 1) * 128], ident_b)
                return ins
            S.op("pe", f, reads=list(rkeys) + ["consts"], writes=[psk[pbank]])
            S.op("act", lambda e: e.activation(dst_fn(), pv[:, 0:nchunks * 128].rearrange("p (c t) -> p c t", c=nchunks), AF.Copy),
                 reads=[psk[pbank]], writes=list(wkeys))

        pbank_rr = [0]

        def next_bank(lo, hi):
            b = lo + pbank_rr[0] % (hi - lo)
            pbank_rr[0] += 1
            return b

        tile_ctr = 0
        fidx = 0
        for k in range(4):
            for tc in range(4):
                cs = (k * 4 + tc) % 2
                full = (k == 3) or (k == 2 and tc == 3)
                for j in range(4):
                    sl = tile_ctr % 3
                    pbk = 6 + tile_ctr % 2
                    if tile_ctr == 0:
                        rms_tile(xw[0, 0:128, :], 0, None)
                    nxt = tile_ctr + 1
                    if nxt < 64:
                        rms_tile(xw[nxt // 16, (nxt % 16) * 128:(nxt % 16 + 1) * 128, :], nxt % 3, None)
                    tile_ctr += 1
                    transpose8(hnb[sl], pbk,
                               lambda cs=cs, j=j: hnT[cs][:, :, j * 128:(j + 1) * 128],
                               [f"hnb{sl}"], [f"hnT{cs}_{j}"])
                hkeys = [f"hnT{cs}_{j}" for j in range(4)]
                if k == 0 and tc == 0:
                    checkpoint("A2")
                for cj in range(4):
                    pb = next_bank(0, 6)

                    def f(e, cj=cj, pb=pb, cs=cs):
                        ins = None
                        for dc in range(8):
                            ins = e.matmul(ps[pb][:], winb[:, dc, (6 + cj) * 128:(7 + cj) * 128], hnT[cs][:, dc, :],
                                           start=(dc == 0), stop=(dc == 7))
                        return ins
                    S.op("pe", f, reads=hkeys + [f"winb{6 + cj}"], writes=[psk[pb]])
                    S.op("act", lambda e, cj=cj, pb=pb, k=k, tc=tc: e.activation(
                        uT[:, cj, k, tc * 512:(tc + 1) * 512], ps[pb][:], AF.Identity, bias=ball[:, 6 + cj:7 + cj]),
                        reads=[psk[pb], "consts"], writes=[f"uT{cj}_{k}"])
                if k == 0 and tc == 0:
                    checkpoint("A3")
                if k == 2 and tc == 3:
                    checkpoint("A3b")
                if not full:
                    continue
                c0 = fidx * 512
                rsl = 0
                S.dma("sp", rct[rsl], ropec_d[:, c0:c0 + 512], writes=[f"rc{rsl}"])
                S.dma("sp", rst[rsl], ropes_d[:, c0:c0 + 512], writes=[f"rsn{rsl}"])
                if fidx == 0:
                    checkpoint("A3c")
                for cj in range(5):
                    pm = next_bank(0, 6)
                    pp = next_bank(0, 6)

                    def fm(e, ch=cj, pb=pm, cs=cs):
                        ins = None
                        for dc in range(8):
                            ins = e.matmul(ps[pb][:], winb[:, dc, ch * 128:(ch + 1) * 128], hnT[cs][:, dc, :],
                                           start=(dc == 0), stop=(dc == 7))
                        return ins
                    S.op("pe", fm, reads=hkeys + [f"winb{cj}"], writes=[psk[pm]])
                    if fidx == 0 and cj == 0:
                        checkpoint("A4m")
                    S.op("pe", lambda e, ch=10 + cj, pb=pp, cs=cs: [e.matmul(
                        ps[pb][:], winb[:, dc, ch * 128:(ch + 1) * 128], hnT[cs][:, dc, :],
                        start=(dc == 0), stop=(dc == 7)) for dc in range(8)][-1],
                        reads=hkeys + [f"winb{10 + cj}"], writes=[psk[pp]])
                    if fidx == 0 and cj == 0:
                        checkpoint("A4p")
                    dst = qTt[:, cj, c0:c0 + 512] if cj < 4 else kTt[:, c0:c0 + 512]
                    dkey = f"qk{cj}_{fidx}"
                    S.op("dve", lambda e, pb=pm, cj=cj, rsl=rsl: e.scalar_tensor_tensor(
                        tA, ps[pb][:], ball[:, cj:cj + 1], rct[rsl], ALU.add, ALU.mult),
                        reads=[psk[pm], f"rc{rsl}", "consts"], writes=["tA"])
                    S.op("dve", lambda e, pb=pp, cj=cj, rsl=rsl: e.scalar_tensor_tensor(
                        tB, ps[pb][:], ball[:, 10 + cj:11 + cj], rst[rsl], ALU.add, ALU.mult),
                        reads=[psk[pp], f"rsn{rsl}", "consts"], writes=["tB"])
                    S.op("pool", lambda e, dst=dst: e.tensor_tensor(dst, tA, tB, ALU.add),
                         reads=["tA", "tB"], writes=[dkey])
                if fidx == 0:
                    checkpoint("A4c")
                for j in range(4):
                    pb = next_bank(0, 6)
                    S.op("pe", lambda e, pb=pb, cs=cs, j=j: [e.matmul(
                        ps[pb][:, 0:128], hnT[cs][:, dc, j * 128:(j + 1) * 128], winb[:, dc, 5 * 128:6 * 128],
                        start=(dc == 0), stop=(dc == 7)) for dc in range(8)][-1],
                        reads=hkeys + ["winb5"], writes=[psk[pb]])
                    S.op("dve", lambda e, pb=pb, jj=fidx * 4 + j: e.tensor_tensor(
                        Vt[:, jj, :], ps[pb][:, 0:128], bvb, ALU.add),
                        reads=[psk[pb], "consts"], writes=[f"V{fidx * 4 + j}"])
                fidx += 1
                if fidx == 1:
                    checkpoint("A4")

        checkpoint("A")
        S.fence()
        AR.top = phaseB_top
        ssb = AR.f32(8 * 256).rearrange("p (h k) -> p h k", h=8)
        pbf = AR.bf(8 * 256).rearrange("p (h k) -> p h k", h=8)
        ptb = AR.bf(16 * 128).rearrange("p (j q) -> p j q", j=16)
        att = AR.f32(512)
        attb = AR.bf(512)
        sm8 = AR.f32(64)
        mx, negm, rsum, esk, den, rden = (sm8[:, i * 8:(i + 1) * 8] for i in range(6))
        ass = sm8[:, 48:49]
        arstd = sm8[:, 49:50]
        pbf2 = AR.bf(8 * 256).rearrange("p (h k) -> p h k", h=8)
        pbfs = [pbf, pbf2]
        rdens = [rden, sm8[:, 56:64]]

        def att_stage1(n):
            b = n % 2
            jt = n + 3
            kc0 = (jt - 1) * 128
            mk = maskt[:, 0:256] if n == 1 else maskt[:, 256:512]
            for h in range(8):
                bank = h // 2
                b0 = 64 * (h // 4)
                S.op("pe", lambda e, h=h, bank=bank, b0=b0: e.matmul(
                    ps[bank][:, (h % 2) * 256:(h % 2) * 256 + 256],
                    qTt[b0:b0 + 64, h % 4, jt * 128:(jt + 1) * 128], kTt[b0:b0 + 64, kc0:kc0 + 256],
                    start=True, stop=True),
                    reads=[f"qk{h % 4}_{jt // 4}", f"qk4_{(jt - 1) // 4}", f"qk4_{jt // 4}"], writes=[psk[bank]])
            for bank in range(4):
                S.op("dve", lambda e, bank=bank: e.scalar_tensor_tensor(
                    ssb[:, 2 * bank:2 * bank + 2, :], ps[bank][:].rearrange("p (h k) -> p h k", h=2), 0.125,
                    mk.unsqueeze(1).to_broadcast([128, 2, 256]), ALU.mult, ALU.add),
                    reads=[psk[bank], "consts"], writes=[f"ssb{bank}"])
            skeys = [f"ssb{i}" for i in range(4)]
            S.op("dve", lambda e: e.reduce_max(mx, ssb, AX.X), reads=skeys, writes=["mx"])
            S.op("dve", lambda e: e.tensor_tensor(negm, mx, sinkb, ALU.max), reads=["mx", "consts"], writes=["negm"])
            S.op("dve", lambda e: e.tensor_scalar(negm, negm, -1.0, None, ALU.mult), reads=["negm"], writes=["negm"])
            for h in range(8):
                S.op("act", lambda e, h=h: e.activation(pbfs[b][:, h, :], ssb[:, h, :], AF.Exp, bias=negm[:, h:h + 1],
                                                        accum_out=rsum[:, h:h + 1]),
                     reads=[f"ssb{h // 2}", "negm"], writes=[f"pbf{b}_{h}", f"rsum{h}"])
            S.op("dve", lambda e: e.tensor_tensor(esk, sinkb, negm, ALU.add), reads=["negm", "consts"], writes=["esk"])
            S.op("act", lambda e: e.activation(esk, esk, AF.Exp), reads=["esk"], writes=["esk"])
            S.op("dve", lambda e: e.tensor_tensor(den, rsum, esk, ALU.add),
                 reads=["esk"] + [f"rsum{h}" for h in range(8)], writes=["den"])
            S.op("dve", lambda e: e.reciprocal(rdens[b], den), reads=["den"], writes=[f"rden{b}"])

        def att_stage2(n):
            b = n % 2
            jt = n + 3
            for half in range(2):
                pv = ps[4 + half][:].bitcast(BF)

                def f(e, half=half, pv=pv):
                    ins = None
                    for i in range(8):
                        h = half * 4 + i // 2
                        kk = i % 2
                        ins = e.transpose(pv[:, i * 128:(i + 1) * 128], pbfs[b][:, h, kk * 128:(kk + 1) * 128], ident_b)
                    return ins
                S.op("pe", f, reads=[f"pbf{b}_{half * 4 + i}" for i in range(4)] + ["consts"], writes=[psk[4 + half]])
                if half == 0:
                    S.op("act", lambda e, pv=pv: e.activation(
                        ptb[:, 0:8, :], pv.rearrange("p (j q) -> p j q", j=8), AF.Copy),
                        reads=[psk[4]], writes=["ptb0"])
                else:
                    S.op("dve", lambda e, pv=pv: e.tensor_copy(
                        ptb[:, 8:16, :], pv.rearrange("p (j q) -> p j q", j=8)),
                        reads=[psk[5]], writes=["ptb1"])

            def fpv(e):
                ins = None
                for h in range(8):
                    kv = h // 4
                    for kk in range(2):
                        ins = e.matmul(ps[6][:, h * 64:(h + 1) * 64], ptb[:, h * 2 + kk, :],
                                       Vt[:, jt - 1 + kk, kv * 64:(kv + 1) * 64], start=(kk == 0), stop=(kk == 1))
                return ins
            S.op("pe", fpv, reads=["ptb0", "ptb1", f"V{jt - 1}", f"V{jt}"], writes=[psk[6]])
            S.op("dve", lambda e: e.tensor_tensor(
                att.rearrange("p (h d) -> p h d", h=8), ps[6][:].rearrange("p (h d) -> p h d", h=8),
                rdens[b].unsqueeze(2).to_broadcast([128, 8, 64]), ALU.mult),
                reads=[psk[6], f"rden{b}"], writes=["att"])
            S.op("dve", lambda e: e.scalar_tensor_tensor(attb, att, 1.0, att, ALU.mult, ALU.mult, accum_out=ass),
                 reads=["att"], writes=["attb", "ass"])
            S.op("dve", lambda e: e.tensor_scalar(arstd, ass, 1.0 / 512, EPS, ALU.mult, ALU.add), reads=["ass"], writes=["arstd"])
            rsqrt_ops(arstd, "arstd")
            S.op("dve", lambda e: e.tensor_scalar(attb, att, arstd, None, ALU.mult), reads=["att", "arstd"], writes=["attb"])
            transpose8(attb, 7, lambda: mixT[:, 0:4, n * 128:(n + 1) * 128], ["attb"], [f"mixA{n}"], nchunks=4)

        att_stage1(0)
        for n in range(17):
            if n + 1 < 17:
                att_stage1(n + 1)
            att_stage2(n)

        checkpoint("B")
        S.fence()
        AR.top = R_A
        LX = AR.bf(33 * 128).rearrange("p (g n) -> p g n", g=33)
        LXp = AR.bf(33 * 128).rearrange("p (g n) -> p g n", g=33)
        LY1 = AR.bf(33 * 128).rearrange("p (g n) -> p g n", g=33)
        assert AR.top <= R_U
        AR.top = R_Q
        ssm_top = AR.top
        LY2 = AR.bf(33 * 128).rearrange("p (g n) -> p g n", g=33)
        tcol = AR.f32(QT)
        pr = {nm: AR.f32(G) for nm in ["lre", "lim", "dt", "a", "th", "rr", "t1", "t2", "lbre", "lbim", "nr",
                                       "den", "cre", "cim", "cimS", "creN", "thm", "fA", "fB", "ki", "ab", "negA", "a2048", "R", "R1536", "Rm512"]}
        fAk = [AR.f32(G) for _ in range(3)]
        fBk = [AR.f32(G) for _ in range(3)]
        carry = AR.f32(2)
        ctmp = AR.f32(2)
        zacc = AR.f32(8)
        zt = AR.f32(2)
        zt2 = AR.f32(2)
        loop_top = AR.top
        C1 = AR.f32(QT)
        S1 = AR.f32(QT)
        bq = AR.f32(QT)
        wv = AR.f32(QT)
        p12_off = AR.top
        P1 = AR.bf(QT)
        P2 = AR.bf(QT)
        psc = arena_t[:, p12_off:p12_off + QT]
        tm_off = AR.top
        tm1 = [AR.f32(512) for _ in range(2)]
        tm2 = [AR.f32(512) for _ in range(2)]
        tmall = arena_t[:, tm_off:tm_off + QT]
        TMK = ["tm10", "tm11", "tm20", "tm21"]
        AR.top = loop_top
        bs1 = AR.f32(G * 16).rearrange("p (g c) -> p g c", g=G)
        bs2 = AR.f32(G * 16).rearrange("p (g c) -> p g c", g=G)
        bt = AR.f32(G * 16).rearrange("p (g c) -> p g c", g=G)
        bt2 = AR.f32(G * 16).rearrange("p (g c) -> p g c", g=G)
        bz = AR.f32(33 * 128)
        bzp = AR.f32(33 * 128)
        csa = AR.f32(512)
        csb = AR.f32(512)
        lyd = AR.f32(512)

        for dst, src in [(pr["lre"], lre_d), (pr["lim"], lim_d), (pr["dt"], lstep_d), (tcol, tcol_d),
                         (bs1.rearrange("p g c -> p (g c)"), bs1_d), (bs2.rearrange("p g c -> p (g c)"), bs2_d),
                         (csa, csa_d), (csb, csb_d)]:
            S.dma("sp", dst, src, writes=["ssmc"], key="setup2")

        def dv(fn, r=("ssmp",), w=("ssmp",), eng="dve"):
            S.op(eng, fn, reads=list(r) + ["ssmc", "consts", "negpi"], writes=list(w))

        p = pr
        dv(lambda e: e.activation(p["dt"], p["dt"], AF.Exp), eng="act")
        dv(lambda e: e.tensor_tensor(p["a"], p["lre"], p["dt"], ALU.mult))
        dv(lambda e: e.tensor_tensor(p["th"], p["lim"], p["dt"], ALU.mult))
        dv(lambda e: e.activation(p["rr"], p["a"], AF.Exp), eng="act")
        dv(lambda e: e.activation(p["R"], p["a"], AF.Exp, scale=float(QT)), eng="act")
        dv(lambda e: e.activation(p["R1536"], p["a"], AF.Exp, scale=1536.0), eng="act")
        dv(lambda e: e.activation(p["Rm512"], p["a"], AF.Exp, scale=-512.0), eng="act")
        dv(lambda e: e.tensor_scalar(p["negA"], p["a"], -1.0, None, ALU.mult))
        dv(lambda e: e.tensor_scalar(p["a2048"], p["a"], float(QT), None, ALU.mult))
        dv(lambda e: e.memset(zt, 0.0))
        dv(lambda e: e.memset(zt2, 0.0))
        dv(lambda e: e.tensor_copy(p["thm"], p["th"]))
        trig_ops(p["thm"], p["ki"].bitcast(I32), p["ab"], p["t1"], p["t2"], "ssmp", "ssmp", "ssmp", "ssmp", "ssmp",
                 extra_reads=["ssmc", "consts", "negpi"])
        dv(lambda e: e.tensor_tensor(p["lbre"], p["rr"], p["t2"], ALU.mult))
        dv(lambda e: e.tensor_tensor(p["lbim"], p["rr"], p["t1"], ALU.mult))
        dv(lambda e: e.tensor_scalar(p["nr"], p["lbre"], -1.0, None, ALU.add))
        dv(lambda e: e.tensor_tensor(p["den"], p["lre"], p["lre"], ALU.mult))
        dv(lambda e: e.tensor_tensor(p["t1"], p["lim"], p["lim"], ALU.mult))
        dv(lambda e: e.tensor_tensor(p["den"], p["den"], p["t1"], ALU.add))
        dv(lambda e: e.reciprocal(p["den"], p["den"]))
        dv(lambda e: e.tensor_tensor(p["cre"], p["nr"], p["lre"], ALU.mult))
        dv(lambda e: e.tensor_tensor(p["t1"], p["lbim"], p["lim"], ALU.mult))
        dv(lambda e: e.tensor_tensor(p["cre"], p["cre"], p["t1"], ALU.add))
        dv(lambda e: e.tensor_tensor(p["cre"], p["cre"], p["den"], ALU.mult))
        dv(lambda e: e.tensor_tensor(p["cim"], p["lbim"], p["lre"], ALU.mult))
        dv(lambda e: e.tensor_tensor(p["t1"], p["nr"], p["lim"], ALU.mult))
        dv(lambda e: e.tensor_tensor(p["cim"], p["cim"], p["t1"], ALU.subtract))
        dv(lambda e: e.tensor_tensor(p["cim"], p["cim"], p["den"], ALU.mult))
        dv(lambda e: e.tensor_scalar(p["cimS"], p["cim"], sgn[:, 0:1], None, ALU.mult))
        dv(lambda e: e.tensor_scalar(p["creN"], p["cre"], sgn[:, 0:1], -1.0, ALU.mult, ALU.mult))
        dv(lambda e: e.tensor_scalar(p["t1"], p["thm"], float(QT), None, ALU.mult))
        trig_ops(p["t1"], p["ki"].bitcast(I32), p["ab"], p["fB"], p["fA"], "ssmp", "ssmp", "ssmp", "ssmp", "ssmp",
                 extra_reads=["ssmc", "consts", "negpi"])
        dv(lambda e: e.tensor_scalar(p["fB"], p["fB"], sgn[:, 0:1], None, ALU.mult))
        for k in range(3):
            dv(lambda e, k=k: e.tensor_scalar(fAk[k], p["fA"], flags[:, k:k + 1], None, ALU.mult))
            dv(lambda e, k=k: e.tensor_scalar(fBk[k], p["fB"], flags[:, k:k + 1], None, ALU.mult))
        bc = lambda t: t.unsqueeze(2).to_broadcast([128, G, 16])
        dv(lambda e: e.tensor_tensor(bt, bs1, bc(p["cre"]), ALU.mult))
        dv(lambda e: e.tensor_tensor(bt2, bs2, bc(p["cimS"]), ALU.mult))
        dv(lambda e: e.tensor_tensor(bt, bt, bt2, ALU.add))
        dv(lambda e: e.tensor_tensor(bt2, bs2, bc(p["creN"]), ALU.mult))
        dv(lambda e: e.tensor_tensor(bs1, bs1, bc(p["cim"]), ALU.mult))
        dv(lambda e: e.tensor_tensor(bt2, bt2, bs1, ALU.add))
        dv(lambda e: e.memset(bz, 0.0))
        dv(lambda e: e.memset(bzp, 0.0))
        for c in range(4):
            dv(lambda e, c=c: e.tensor_copy(
                bz[:, 1024 * c:1024 * c + 1152].rearrange("p (j w) -> p j w", w=144)[:, :, 0:16], bt[:, 8 * c:8 * c + 8, :]))
            dv(lambda e, c=c: e.tensor_copy(
                bzp[:, 1024 * c:1024 * c + 1152].rearrange("p (j w) -> p j w", w=144)[:, :, 0:16], bt2[:, 8 * c:8 * c + 8, :]))
        for src, dstL, nm in [(bz, LX, "LX"), (bzp, LXp, "LXp")]:
            for g4 in range(8):
                pb = g4 % 2

                def f(e, src=src, g4=g4, pb=pb):
                    ins = None
                    for i in range(4):
                        g = g4 * 4 + i
                        ins = e.transpose(ps[pb][:, i * 128:(i + 1) * 128], src[:, g * 128:(g + 1) * 128], ident_f)
                    return ins
                S.op("pe", f, reads=["ssmp", "consts"], writes=[psk[pb]])
                S.op("act", lambda e, dstL=dstL, g4=g4, pb=pb: e.activation(
                    dstL[:, g4 * 4:g4 * 4 + 4, :], ps[pb][:].rearrange("p (g n) -> p g n", g=4), AF.Copy),
                    reads=[psk[pb]], writes=[nm])
        cs3a = csa.rearrange("p (c n) -> p c n", c=4)
        cs3b = csb.rearrange("p (c n) -> p c n", c=4)
        dv(lambda e: e.tensor_scalar(cs3a[:, :, 64:128], cs3a[:, :, 64:128], -1.0, None, ALU.mult))
        dv(lambda e: e.tensor_scalar(csb, csb, -1.0, None, ALU.mult))
        for src, dstL, nm in [(csa, LY1, "LY1"), (csb, LY2, "LY2")]:
            S.op("pool", lambda e, dstL=dstL: e.memset(dstL, 0.0), reads=[], writes=[nm])

            def f(e, src=src):
                ins = None
                for c in range(4):
                    ins = e.transpose(ps[2][:, c * 128:(c + 1) * 128], src[:, c * 128:(c + 1) * 128], ident_f)
                return ins
            S.op("pe", f, reads=["ssmp", "consts"], writes=[psk[2]])
            S.op("dve", lambda e: e.tensor_copy(lyd, ps[2][:]), reads=[psk[2]], writes=["lyd"])
            for c in range(4):
                S.op("dve", lambda e, c=c, dstL=dstL: e.tensor_copy(
                    dstL.rearrange("p g n -> p (g n)")[:, 1024 * c:1024 * c + 1152].rearrange("p (j w) -> p j w", w=144)[:, :, 0:16],
                    lyd[:, c * 128:(c + 1) * 128].rearrange("p (j w) -> p j w", w=16)),
                    reads=["lyd"], writes=[nm])

        S.fence()
        checkpoint("C0")
        yT = uT
        ADD_ENG = os.environ.get("KADD", "pool")
        P2_ENG = os.environ.get("KP2", "dve")
        for g in range(G):
            c = g // 8
            gi = g % 8
            S.op("act", lambda e, g=g: e.activation(tmall, tcol, AF.Exp, bias=p["a2048"][:, g:g + 1], scale=p["negA"][:, g:g + 1]),
                 reads=["ssmp", "ssmc"], writes=TMK)
            S.op("act", lambda e: e.activation(ctmp[:, 1:2], halfpi, AF.Sin), reads=["negpi"], writes=["actwarm"])
            wvi = wv.bitcast(I32)
            SPL = 1536
            pieces = [(slice(0, SPL), ["bqc0", "bqc1", "bqc2"]), (slice(SPL, QT), ["bqc2", "bqc3"])]
            for hh in range(2):
                cs_, bcs = pieces[hh]
                bk, wk = f"bqh{hh}", f"wvh{hh}"
                S.op("dve", lambda e, g=g, cs_=cs_: e.tensor_scalar(bq[:, cs_], tcol[:, cs_], p["thm"][:, g:g + 1], None, ALU.mult),
                     reads=["ssmp", "ssmc"], writes=[bk, "bq"] + bcs)
                S.op("dve", lambda e, cs_=cs_: e.tensor_scalar(wvi[:, cs_], bq[:, cs_], 1.0 / TWO_PI, None, ALU.mult),
                     reads=[bk], writes=[wk, "wv"])
                S.op("dve", lambda e, cs_=cs_: e.scalar_tensor_tensor(bq[:, cs_], wvi[:, cs_], -TWO_PI, bq[:, cs_], ALU.mult, ALU.add),
                     reads=[wk, bk], writes=[bk])
                S.op("dve", lambda e, cs_=cs_: e.tensor_scalar(bq[:, cs_], bq[:, cs_], PI, -PI, ALU.min, ALU.max),
                     reads=[bk], writes=[bk])
                S.op("act", lambda e, cs_=cs_: e.activation(S1[:, cs_], bq[:, cs_], AF.Sin), reads=[bk], writes=[f"S1h{hh}", "S1"])
                S.op("act", lambda e, cs_=cs_: e.activation(wv[:, cs_], bq[:, cs_], AF.Abs), reads=[bk, wk], writes=[wk])
                S.op("act", lambda e, cs_=cs_: e.activation(C1[:, cs_], wv[:, cs_], AF.Sin, bias=halfpi, scale=-1.0),
                     reads=[wk, "negpi"], writes=[f"C1h{hh}", "C1"])
            S.op("dve", lambda e: e.memset(carry, 0.0), reads=[], writes=["carry"])
            for hh in range(2):
                cs_, bcs = pieces[hh]
                bk, wk = f"bqh{hh}", f"wvh{hh}"
                S.op("dve", lambda e, cs_=cs_: e.tensor_tensor(wv[:, cs_], tmall[:, cs_], S1[:, cs_], ALU.mult),
                     reads=TMK + [f"S1h{hh}"], writes=[wk, "wv"])
                S.op("dve", lambda e, cs_=cs_: e.tensor_tensor(bq[:, cs_], tmall[:, cs_], C1[:, cs_], ALU.mult),
                     reads=TMK + [f"C1h{hh}"], writes=[bk, "bq"] + bcs)
            zts = [zt, zt2]

            def acc_ops(k, ntc, g=g, c=c):
                for tc in range(ntc):
                    sl = tc % 2
                    cols = slice(tc * 512, (tc + 1) * 512)
                    S.op("pe", lambda e, cols=cols, sl=sl: e.matmul(
                        ps[sl][:], LX[:, g, :], uT[:, c, k, cols], start=True, stop=True),
                        reads=["LX", f"uT{c}_{k}"], writes=[psk[sl]])
                    S.op("pe", lambda e, cols=cols: e.matmul(
                        ps[2][:], LXp[:, g, :], uT[:, c, k, cols], start=True, stop=True),
                        reads=["LXp", f"uT{c}_{k}"], writes=[psk[2]])
                    S.op("dve", lambda e, cols=cols, sl=sl, tc=tc: e.scalar_tensor_tensor(
                        tm1[sl], ps[sl][:], 1.0, bq[:, cols], ALU.mult, ALU.mult, accum_out=zacc[:, 2 * tc:2 * tc + 1]),
                        reads=[psk[sl], "bq"], writes=[f"tm1{sl}", "zacc"])
                    S.op("dve", lambda e, cols=cols, sl=sl, tc=tc: e.scalar_tensor_tensor(
                        tm2[sl], ps[2][:], 1.0, wv[:, cols], ALU.mult, ALU.mult, accum_out=zacc[:, 2 * tc + 1:2 * tc + 2]),
                        reads=[psk[2], "wv"], writes=[f"tm2{sl}", "zacc"])

            def reduce_to(zi, ncols):
                S.op("dve", lambda e: e.reduce_sum(zts[zi][:, 1:2], zacc[:, 0:ncols], AX.X), reads=["zacc"], writes=[f"zt{zi}"])

            def reframe_from(zi, k, g=g):
                S.op("pe", lambda e: e.matmul(ps[2][:, 510:512], swapj, zts[zi][:, 0:2], start=True, stop=True),
                     reads=[f"zt{zi}", "consts"], writes=[psk[2]])
                S.op("dve", lambda e: e.tensor_scalar(ctmp[:, 0:1], zts[zi][:, 1:2], fAk[k][:, g:g + 1], None, ALU.mult),
                     reads=[f"zt{zi}", "ssmp"], writes=["ctmp"])
                S.op("dve", lambda e: e.scalar_tensor_tensor(
                    carry[:, 0:1], ps[2][:, 511:512], fBk[k][:, g:g + 1], ctmp[:, 0:1], ALU.mult, ALU.add),
                    reads=[psk[2], "ctmp", "ssmp"], writes=["carry"])

            acc_ops(0, 4)
            reduce_to(0, 8)
            acc_ops(1, 4)
            reframe_from(0, 0)
            reduce_to(1, 8)
            S.op("dve", lambda e, g=g: e.scalar_tensor_tensor(
                zt2[:, 1:2], carry[:, 0:1], p["R"][:, g:g + 1], zt2[:, 1:2], ALU.mult, ALU.add),
                reads=["carry", "zt1", "ssmp"], writes=["zt1"])
            for k in range(2, 4):
                def mults(tc, g=g, c=c, k=k):
                    sl = tc % 2
                    cols = slice(tc * 512, (tc + 1) * 512)
                    S.op("pe", lambda e: e.matmul(
                        ps[sl][:], LX[:, g, :], uT[:, c, k, cols], start=True, stop=True),
                        reads=["LX", f"uT{c}_{k}"], writes=[psk[sl]])
                    S.op("pe", lambda e: e.matmul(
                        ps[2][:], LXp[:, g, :], uT[:, c, k, cols], start=True, stop=True),
                        reads=["LXp", f"uT{c}_{k}"], writes=[psk[2]])
                    S.op("dve", lambda e: e.tensor_tensor(tm1[sl], ps[sl][:], C1[:, cols], ALU.mult),
                         reads=[psk[sl], "C1"], writes=[f"tm1{sl}"])
                    S.op("dve", lambda e: e.tensor_tensor(tm2[sl], ps[2][:], S1[:, cols], ALU.mult),
                         reads=[psk[2], "S1"], writes=[f"tm2{sl}"])
                    S.op(ADD_ENG if tc < 3 else "dve", lambda e: e.tensor_tensor(bq[:, cols], tm1[sl], tm2[sl], ALU.add),
                         reads=[f"tm1{sl}", f"tm2{sl}"], writes=[f"bqc{tc}", "bq"])

                def cscan(tc, g=g, from_carry=False):
                    cols = slice(tc * 512, (tc + 1) * 512)
                    init = carry[:, 0:1] if (tc == 0 or from_carry) else wv[:, tc * 512 - 1:tc * 512]
                    S.op("dve", lambda e: e.tensor_tensor_scan(
                        wv[:, cols], p["rr"][:, g:g + 1].to_broadcast([128, 512]), bq[:, cols], init, ALU.mult, ALU.add),
                        reads=[f"bqc{tc}", "carry", "wv", "ssmp"], writes=["wv"])
                if k == 2:
                    acc_ops(2, 3)
                    reframe_from(1, 1)
                    reduce_to(0, 6)
                    S.op("dve", lambda e, g=g: e.tensor_scalar(ctmp[:, 0:1], carry[:, 0:1], p["R1536"][:, g:g + 1], None, ALU.mult),
                         reads=["carry", "ssmp"], writes=["ctmp"])
                    S.op("dve", lambda e, g=g: e.scalar_tensor_tensor(
                        carry[:, 0:1], zt[:, 1:2], p["Rm512"][:, g:g + 1], ctmp[:, 0:1], ALU.mult, ALU.add),
                        reads=["zt0", "ctmp", "ssmp"], writes=["carry"])
                    mults(3)
                    cscan(3, from_carry=True)
                else:
                    mults(0)
                    mults(1)
                    mults(2)
                    cscan(0)
                    mults(3)
                    cscan(1)
                    cscan(2)
                    cscan(3)
                if k < 3:
                    S.op("pe", lambda e: e.matmul(ps[2][:, 510:512], swapj, wv[:, QT - 2:QT], start=True, stop=True),
                         reads=["wv", "consts"], writes=[psk[2]])
                    S.op("dve", lambda e, g=g, k=k: e.tensor_scalar(ctmp[:, 0:1], wv[:, QT - 1:QT], fAk[k][:, g:g + 1], None, ALU.mult),
                         reads=["wv", "ssmp"], writes=["ctmp"])
                    S.op("dve", lambda e, g=g, k=k: e.scalar_tensor_tensor(
                        carry[:, 0:1], ps[2][:, 511:512], fBk[k][:, g:g + 1], ctmp[:, 0:1], ALU.mult, ALU.add),
                        reads=[psk[2], "ctmp", "ssmp"], writes=["carry"])
                if k >= 2:
                    lo = QT - 128 if k == 2 else 0
                    S.op("dve", lambda e, lo=lo: e.tensor_tensor(P1[:, lo:QT], wv[:, lo:QT], C1[:, lo:QT], ALU.mult),
                         reads=["wv", "C1"], writes=["P1"])
                    S.op(P2_ENG, lambda e, lo=lo: e.tensor_tensor(P2[:, lo:QT], wv[:, lo:QT], S1[:, lo:QT], ALU.mult),
                         reads=["wv", "S1"], writes=["P2"])
                    if k == 2:
                        def fh(e, g=g, gi=gi):
                            e.matmul(ps[3][:, 0:128], LY1[:, g, :], P1[:, QT - 128:QT], start=(gi == 0), stop=False)
                            return e.matmul(ps[3][:, 0:128], LY2[:, g, :], P2[:, QT - 128:QT], start=False, stop=(gi == 7))
                        S.op("pe", fh, reads=["P1", "P2", "LY1", "LY2"], writes=[psk[3]])
                    else:
                        def fo(e, g=g, gi=gi):
                            ins = None
                            for tc in range(4):
                                cols = slice(tc * 512, (tc + 1) * 512)
                                e.matmul(ps[4 + tc][:], LY1[:, g, :], P1[:, cols], start=(gi == 0), stop=False)
                                ins = e.matmul(ps[4 + tc][:], LY2[:, g, :], P2[:, cols], start=False, stop=(gi == 7))
                            return ins
                        S.op("pe", fo, reads=["P1", "P2", "LY1", "LY2"], writes=[psk[4], psk[5], psk[6], psk[7]])
            if gi == 7:
                yc = uT[:, c, 0:2, :].rearrange("p k t -> p (k t)").bitcast(F32)
                yh = uT[:, c, 2, 0:256].bitcast(F32)
                S.op("dve", lambda e, c=c, yh=yh: e.scalar_tensor_tensor(
                    yh, uT[:, c, 2, QT - 128:QT], dsk[:, c:c + 1], ps[3][:, 0:128], ALU.mult, ALU.add),
                    reads=[psk[3], f"uT{c}_2", "consts"], writes=[f"yhalo{c}"])
                for tc in range(4):
                    S.op("dve", lambda e, c=c, tc=tc, yc=yc: e.scalar_tensor_tensor(
                        yc[:, tc * 512:(tc + 1) * 512], uT[:, c, 3, tc * 512:(tc + 1) * 512], dsk[:, c:c + 1],
                        ps[4 + tc][:], ALU.mult, ALU.add),
                        reads=[psk[4 + tc], f"uT{c}_3", f"uT{c}_0", f"uT{c}_1", "consts"], writes=[f"yT{c}"])

        checkpoint("C")
        S.fence()
        AR.top = R_A
        woutb = AR.bf(8 * D).rearrange("p (c n) -> p c n", c=8)
        wglub = AR.bf(4 * 512).rearrange("p (c n) -> p c n", c=4)
        wstD = AR.f32(8 * 256).rearrange("p (c n) -> p c n", c=8)
        assert AR.top <= R_U
        AR.top = R_Q
        hn2T = AR.bf(8 * NTOK).rearrange("p (c t) -> p c t", c=8)
        R_H = AR.top
        CH = 512
        gx2 = AR.f32(4 * CH).rearrange("p (c t) -> p c t", c=4)
        gyg = AR.f32(4 * CH).rearrange("p (c t) -> p c t", c=4)
        gyb = AR.bf(4 * CH).rearrange("p (c t) -> p c t", c=4)
        gsg = AR.f32(4 * CH).rearrange("p (c t) -> p c t", c=4)
        grs = AR.f32(CH)
        assert AR.top <= NW
        AR.top = R_H
        xt2 = [AR.f32(1024) for _ in range(2)]
        ht2 = [AR.f32(1024) for _ in range(2)]
        hb2 = [AR.bf(1024) for _ in range(2)]
        st2 = [AR.f32(2) for _ in range(2)]
        assert AR.top <= NW
        wglu_v = wglu_d.rearrange("(c p) n -> p c n", p=128)
        for hf in range(2):
            S.dma("sp", wstD[:, 0:4, :], wglu_v[:, :, hf * 256:(hf + 1) * 256], writes=["wstD"])
            S.op("pool", lambda e, hf=hf: e.tensor_copy(wglub[:, :, hf * 256:(hf + 1) * 256], wstD[:, 0:4, :]),
                 reads=["wstD"], writes=["wglub"])
        wout_v = wout_d.rearrange("(c p) n -> p c n", p=128)
        for hf in range(4):
            S.dma("sp", wstD, wout_v[:, :, hf * 256:(hf + 1) * 256], writes=["wstD"])
            S.op("pool", lambda e, hf=hf: e.tensor_tensor(
                woutb[:, :, hf * 256:(hf + 1) * 256], wstD, gmix.unsqueeze(2).to_broadcast([128, 8, 256]), ALU.mult),
                reads=["wstD", "consts"], writes=["woutb"])
        chunks = [(0, 128)] + [(128 + i * CH, CH) for i in range(QT // CH)]
        for ci, (t0, n) in enumerate(chunks):
            def ysrc(c, ci=ci, n=n):
                if ci == 0:
                    return uT[:, c, 2, 0:256].bitcast(F32)
                yc = uT[:, c, 0:2, :].rearrange("p k t -> p (k t)").bitcast(F32)
                return yc[:, (ci - 1) * CH:ci * CH]
            ykeys = [f"yhalo{c}" if ci == 0 else f"yT{c}" for c in range(4)]
            for c in range(4):
                S.op("dve", lambda e, c=c, n=n, ysrc=ysrc: e.tensor_tensor(gx2[:, c, 0:n], ysrc(c), ysrc(c), ALU.mult),
                     reads=[ykeys[c]], writes=[f"gx2{c}"])
                S.op("dve", lambda e, c=c, n=n: e.tensor_scalar(gx2[:, c, 0:n], gx2[:, c, 0:n], 0.044715, 1.0, ALU.mult, ALU.add),
                     reads=[f"gx2{c}"], writes=[f"gx2{c}"])
                S.op("dve", lambda e, c=c, n=n, ysrc=ysrc: e.tensor_tensor(gx2[:, c, 0:n], gx2[:, c, 0:n], ysrc(c), ALU.mult),
                     reads=[f"gx2{c}", ykeys[c]], writes=[f"gx2{c}"])
                S.op("act", lambda e, c=c, n=n: e.activation(gx2[:, c, 0:n], gx2[:, c, 0:n], AF.Sigmoid, scale=1.5957691216057308),
                     reads=[f"gx2{c}"], writes=[f"gx2{c}"])
                S.op("dve", lambda e, c=c, n=n, ysrc=ysrc: e.tensor_tensor(gyg[:, c, 0:n], gx2[:, c, 0:n], ysrc(c), ALU.mult),
                     reads=[f"gx2{c}", ykeys[c]], writes=[f"gyg{c}"])
                S.op("act", lambda e, c=c, n=n: e.activation(gyb[:, c, 0:n], gyg[:, c, 0:n], AF.Copy), reads=[f"gyg{c}"], writes=[f"gyb{c}"])
            for nc_ in range(4):
                pb = nc_ % 2
                S.op("pe", lambda e, nc_=nc_, pb=pb, n=n: [e.matmul(
                    ps[pb][:, 0:n], wglub[:, kc, nc_ * 128:(nc_ + 1) * 128], gyb[:, kc, 0:n],
                    start=(kc == 0), stop=(kc == 3)) for kc in range(4)][-1],
                    reads=["wglub"] + [f"gyb{c}" for c in range(4)], writes=[psk[pb]])
                S.op("act", lambda e, nc_=nc_, pb=pb, n=n: e.activation(
                    gsg[:, nc_, 0:n], ps[pb][:, 0:n], AF.Sigmoid, bias=bglu[:, nc_:nc_ + 1]),
                    reads=[psk[pb], "consts"], writes=[f"gsg{nc_}"])
                S.op("dve", lambda e, nc_=nc_, n=n: e.tensor_tensor(gsg[:, nc_, 0:n], gsg[:, nc_, 0:n], gyg[:, nc_, 0:n], ALU.mult),
                     reads=[f"gsg{nc_}", f"gyg{nc_}"], writes=[f"gsg{nc_}"])
                S.op("dve", lambda e, nc_=nc_, n=n: e.tensor_tensor(gx2[:, nc_, 0:n], gsg[:, nc_, 0:n], gsg[:, nc_, 0:n], ALU.mult),
                     reads=[f"gsg{nc_}"], writes=[f"gx2{nc_}"])
            S.op("pe", lambda e, n=n: [e.matmul(ps[2][:, 0:n], onesm, gx2[:, kc, 0:n], start=(kc == 0), stop=(kc == 3))
                                       for kc in range(4)][-1],
                 reads=["consts"] + [f"gx2{c}" for c in range(4)], writes=[psk[2]])
            S.op("dve", lambda e, n=n: e.tensor_scalar(grs[:, 0:n], ps[2][:, 0:n], 1.0 / 512, EPS, ALU.mult, ALU.add),
                 reads=[psk[2]], writes=["grs"])
            rsqrt_ops(grs[:, 0:n], "grs")
            for nc_ in range(4):
                S.op("dve", lambda e, nc_=nc_, n=n, t0=t0: e.tensor_tensor(
                    mixT[:, 4 + nc_, t0:t0 + n], gsg[:, nc_, 0:n], grs[:, 0:n], ALU.mult),
                    reads=[f"gsg{nc_}", "grs"], writes=[f"mixS{ci}"])
        S.fence()

        def op_stageA(n):
            sl = n % 2
            src = xw[2, QT - 128:QT, :] if n == 0 else xw[3, (n - 1) * 128:n * 128, :]
            S.dma("sp", xt2[sl], src, writes=[f"xt2{sl}"])
            ci = 0 if n == 0 else 1 + (n - 1) // 4
            for hf in range(2):
                S.op("pe", lambda e, n=n, hf=hf: [e.matmul(
                    ps[4 + hf][:], mixT[:, f, n * 128:(n + 1) * 128], woutb[:, f, hf * 512:(hf + 1) * 512],
                    start=(f == 0), stop=(f == 7)) for f in range(8)][-1],
                    reads=[f"mixA{n}", f"mixS{ci}", "woutb"], writes=[psk[4 + hf]])
                S.op("dve", lambda e, sl=sl, hf=hf: e.tensor_tensor(
                    ht2[sl][:, hf * 512:(hf + 1) * 512], ps[4 + hf][:], xt2[sl][:, hf * 512:(hf + 1) * 512], ALU.add),
                    reads=[psk[4 + hf], f"xt2{sl}"], writes=[f"ht2{sl}"])
            S.dma("sp", hs_d[n * 128:(n + 1) * 128, :], ht2[sl], reads=[f"ht2{sl}"], writes=[f"hs{n}"], key=f"hsw{sl}")
            S.op("dve", lambda e, sl=sl: e.scalar_tensor_tensor(hb2[sl], ht2[sl], 1.0, ht2[sl], ALU.mult, ALU.mult,
                                                               accum_out=st2[sl][:, 0:1]),
                 reads=[f"ht2{sl}"], writes=[f"hb2{sl}", f"st2{sl}"])
            S.op("dve", lambda e, sl=sl: e.tensor_scalar(st2[sl][:, 1:2], st2[sl][:, 0:1], 1.0 / D, EPS, ALU.mult, ALU.add),
                 reads=[f"st2{sl}"], writes=[f"rs2{sl}"])
            rsqrt_ops(st2[sl][:, 1:2], f"rs2{sl}")
            S.op("dve", lambda e, sl=sl: e.tensor_scalar(hb2[sl], ht2[sl], st2[sl][:, 1:2], None, ALU.mult),
                 reads=[f"ht2{sl}", f"rs2{sl}"], writes=[f"hb2{sl}"])

        def op_stageB(n):
            sl = n % 2
            transpose8(hb2[sl], 6 + (n % 2), lambda n=n: hn2T[:, :, n * 128:(n + 1) * 128], [f"hb2{sl}"], [f"hn2T{n}"])

        op_stageA(0)
        for n in range(17):
            if n + 1 < 17:
                op_stageA(n + 1)
            op_stageB(n)
        checkpoint("D")
        S.fence()
        AR.top = R_A
        actT = AR.bf(NFC * QT).rearrange("p (f t) -> p f t", f=NFC)
        assert AR.top <= R_Q
        AR.top = R_H
        wus = [AR.f32(8 * 256).rearrange("p (c n) -> p c n", c=8) for _ in range(1)]
        wub = [AR.bf(8 * 256).rearrange("p (c n) -> p c n", c=8) for _ in range(2)]
        Gb = [AR.f32(QT + 2), None]
        Vb = [AR.f32(QT + 2), None]
        assert AR.top <= NW
        AR.top = R_MIX
        Gb[1] = AR.f32(QT + 2)
        Vb[1] = AR.f32(QT + 2)
        accg = AR.f32(QT)
        accv = [AR.f32(QT), None]
        assert AR.top <= persist_top
        AR.top = R_A + NFC * QT // 2
        assert R_Q - AR.top >= 0
        gap = R_Q - AR.top
        accv[1] = None
        flag2 = flags[:, 2:3]
        cw3 = cw.rearrange("p (c k) -> p c k", k=3)
        hkeys_all = [f"hn2T{n}" for n in range(17)]
        def emit_w(fc):
            sl = fc % 2
            S.dma("sp", wus[0].rearrange("p c n -> p (c n)"), wup_d[fc], writes=["wus0"])
            for dc in range(8):
                S.op("act", lambda e, sl=sl, dc=dc: e.activation(wub[sl][:, dc, :], wus[0][:, dc, :], AF.Copy, scale=g2[:, dc:dc + 1]),
                     reads=["wus0", "consts"], writes=[f"wub{sl}"])

        def emit_mm(fc):
            sl = fc % 2
            if fc == 0:
                emit_w(0)
            pend = []
            if fc + 1 < NFC:
                S.dma("sp", wus[0].rearrange("p c n -> p (c n)"), wup_d[fc + 1], writes=["wus0"])
                pend = list(range(8))
            for gv in range(2):
                dstb = Gb[sl] if gv == 0 else Vb[sl]
                bkey = f"Gb{sl}" if gv == 0 else f"Vb{sl}"
                wcols = slice(gv * 128, (gv + 1) * 128)
                pbh = 4 + gv
                S.op("pe", lambda e, sl=sl, wcols=wcols, pbh=pbh: [e.matmul(
                    ps[pbh][:, 0:2], wub[sl][:, dc, wcols], hn2T[:, dc, 126:128], start=(dc == 0), stop=(dc == 7))
                    for dc in range(8)][-1],
                    reads=[f"wub{sl}", "hn2T0"], writes=[psk[pbh]])
                S.op("act", lambda e, dstb=dstb, pbh=pbh: e.activation(dstb[:, 0:2], ps[pbh][:, 0:2], AF.Copy, scale=flag2),
                     reads=[psk[pbh], "consts", bkey], writes=[bkey + "h"])
                for tc in range(4):
                    pb = gv * 2 + tc % 2
                    S.op("pe", lambda e, sl=sl, wcols=wcols, pb=pb, tc=tc: [e.matmul(
                        ps[pb][:], wub[sl][:, dc, wcols], hn2T[:, dc, 128 + tc * 512:128 + (tc + 1) * 512],
                        start=(dc == 0), stop=(dc == 7)) for dc in range(8)][-1],
                        reads=[f"wub{sl}"] + hkeys_all[1 + tc * 4:5 + tc * 4], writes=[psk[pb]])
                    S.op("act", lambda e, dstb=dstb, pb=pb, tc=tc: e.activation(
                        dstb[:, 2 + tc * 512:2 + (tc + 1) * 512], ps[pb][:], AF.Copy),
                        reads=[psk[pb]], writes=[bkey])
                    if pend:
                        dc = pend.pop(0)
                        nsl = (fc + 1) % 2
                        S.op("act", lambda e, nsl=nsl, dc=dc: e.activation(wub[nsl][:, dc, :], wus[0][:, dc, :], AF.Copy, scale=g2[:, dc:dc + 1]),
                             reads=["wus0", "consts"], writes=[f"wub{nsl}"])
        def emit_conv(fc):
            sl = fc % 2
            for gv in range(2):
                dstb = Gb[sl] if gv == 0 else Vb[sl]
                bkey = f"Gb{sl}" if gv == 0 else f"Vb{sl}"
                acc = accg if gv == 0 else accv[0]
                akey = "accg" if gv == 0 else "accv"
                ch = fc + gv * NFC
                S.op("dve", lambda e, acc=acc, dstb=dstb, ch=ch: e.tensor_scalar(
                    acc, dstb[:, 2:QT + 2], cw3[:, ch, 2:3], cb[:, ch:ch + 1], ALU.mult, ALU.add),
                    reads=[bkey, "consts"], writes=[akey])
                S.op("dve", lambda e, acc=acc, dstb=dstb, ch=ch: e.scalar_tensor_tensor(
                    acc, dstb[:, 1:QT + 1], cw3[:, ch, 1:2], acc, ALU.mult, ALU.add),
                    reads=[bkey, bkey + "h", akey, "consts"], writes=[akey])
                if gv == 0:
                    S.op("dve", lambda e, acc=acc, dstb=dstb, ch=ch: e.scalar_tensor_tensor(
                        acc, dstb[:, 0:QT], cw3[:, ch, 0:1], acc, ALU.mult, ALU.add),
                        reads=[bkey, bkey + "h", akey, "consts"], writes=[akey])
                else:
                    S.op("dve", lambda e, acc=acc, dstb=dstb, ch=ch: e.scalar_tensor_tensor(
                        acc, dstb[:, 0:QT], cw3[:, ch, 0:1], acc, ALU.mult, ALU.add),
                        reads=[bkey, bkey + "h", akey, "consts"], writes=[akey])
        def emit_fin(fc):
            sl = fc % 2
            S.op("act", lambda e, sl=sl: e.activation(Gb[sl][:, 2:QT + 2], accg, AF.Silu), reads=["accg", f"Gb{sl}"], writes=[f"Gb{sl}"])
            S.op("dve", lambda e, fc=fc, sl=sl: e.tensor_tensor(actT[:, fc, :], Gb[sl][:, 2:QT + 2], accv[0], ALU.mult),
                 reads=[f"Gb{sl}", "accv"], writes=[f"actT{fc}"])

        for fc in range(NFC + 1):
            if fc < NFC:
                emit_mm(fc)
            if fc >= 1:
                emit_fin(fc - 1)
            if fc < NFC:
                emit_conv(fc)

        checkpoint("E1")
        S.fence()
        AR.top = R_Q
        wdb = AR.bf(NFC * D).rearrange("p (f n) -> p f n", f=NFC)
        wds = [AR.f32(2 * D).rearrange("p (f n) -> p f n", f=2) for _ in range(2)]
        lnf = AR.f32(D)
        AR.top = R_MIX
        hr = [AR.f32(D) for _ in range(2)]
        ob = [AR.f32(D) for _ in range(2)]
        osq = AR.bf(D)
        st3 = [AR.f32(2) for _ in range(2)]
        assert AR.top <= persist_top
        S.dma("sp", lnf, lnf_d, writes=["lnf"])
        wd_v = wdown_d.rearrange("(f p) n -> p f n", p=128)
        for f2 in range(NFC // 2):
            sl = f2 % 2
            S.dma("sp", wds[sl], wd_v[:, 2 * f2:2 * f2 + 2, :], writes=[f"wds{sl}"])
            if f2 % 2 == 0:
                S.op("pool", lambda e, sl=sl, f2=f2: e.tensor_copy(wdb[:, 2 * f2:2 * f2 + 2, :], wds[sl]),
                     reads=[f"wds{sl}"], writes=[f"wdb{f2}"])
            else:
                S.op("act", lambda e, sl=sl, f2=f2: e.activation(wdb[:, 2 * f2:2 * f2 + 2, :], wds[sl], AF.Copy),
                     reads=[f"wds{sl}"], writes=[f"wdb{f2}"])
        for n in range(16):
            sl = n % 2
            S.dma("sp", hr[sl], hs_d[(n + 1) * 128:(n + 2) * 128, :], reads=[f"hs{n + 1}"], writes=[f"hr{sl}"])
            wkeys = [f"wdb{f2}" for f2 in range(NFC // 2)]
            if n == 0:
                for f2 in range(NFC // 2):
                    for hf in range(2):
                        pb = hf
                        S.op("pe", lambda e, f2=f2, hf=hf, pb=pb: [e.matmul(
                            ps[pb][:], actT[:, f, 0:128], wdb[:, f, hf * 512:(hf + 1) * 512],
                            start=(f == 0), stop=(f == NFC - 1)) for f in (2 * f2, 2 * f2 + 1)][-1],
                            reads=[f"wdb{f2}", f"actT{2 * f2}", f"actT{2 * f2 + 1}"], writes=[psk[pb]])
            for hf in range(2):
                pb = (n % 2) * 2 + hf
                if n > 0:
                    S.op("pe", lambda e, n=n, hf=hf, pb=pb: [e.matmul(
                        ps[pb][:], actT[:, f, n * 128:(n + 1) * 128], wdb[:, f, hf * 512:(hf + 1) * 512],
                        start=(f == 0), stop=(f == NFC - 1)) for f in range(NFC)][-1],
                        reads=wkeys + [f"actT{f}" for f in range(NFC)], writes=[psk[pb]])
                S.op("dve", lambda e, sl=sl, hf=hf, pb=pb: e.tensor_tensor(
                    ob[sl][:, hf * 512:(hf + 1) * 512], ps[pb][:], hr[sl][:, hf * 512:(hf + 1) * 512], ALU.add),
                    reads=[psk[pb], f"hr{sl}"], writes=[f"ob{sl}"])
            S.op("dve", lambda e, sl=sl: e.scalar_tensor_tensor(osq, ob[sl], 1.0, ob[sl], ALU.mult, ALU.mult,
                                                               accum_out=st3[sl][:, 0:1]),
                 reads=[f"ob{sl}"], writes=["osq", f"st3{sl}"])
            S.op("dve", lambda e, sl=sl: e.tensor_scalar(st3[sl][:, 1:2], st3[sl][:, 0:1], 1.0 / D, EPS, ALU.mult, ALU.add),
                 reads=[f"st3{sl}"], writes=[f"rs3{sl}"])
            rsqrt_ops(st3[sl][:, 1:2], f"rs3{sl}")
            S.op("dve", lambda e, sl=sl: e.scalar_tensor_tensor(ob[sl], ob[sl], st3[sl][:, 1:2], lnf, ALU.mult, ALU.mult),
                 reads=[f"ob{sl}", f"rs3{sl}", "lnf"], writes=[f"ob{sl}"])
            S.dma("sp", out_d[n * 128:(n + 1) * 128, :], ob[sl], reads=[f"ob{sl}"], writes=[f"out{n}"], key=f"outw{sl}")
        S.fence()
    except _Stop:
        pass

    dkeys = list(S.dcnt.keys())
    dsem = {k: es.enter_context(nc.semaphore(f"dsem{i}")) for i, k in enumerate(dkeys)}

    def handle(h):
        return esem[h] if isinstance(h, str) else dsem[h]

    def run(name, eng):
        for waits, fn, inc in S.q[name]:
            for h, v in waits:
                eng.wait_ge(handle(h), v)
            if fn is None:
                continue
            ins = fn(eng)
            if isinstance(ins, (list, tuple)):
                ins = ins[-1]
            ins.then_inc(handle(inc), 1 if isinstance(inc, str) else 16)

    with nc.Block() as block:
        @block.tensor
        def _(e):
            run("pe", e)

        @block.scalar
        def _(e):
            run("act", e)

        @block.vector
        def _(e):
            run("dve", e)

        @block.gpsimd
        def _(e):
            run("pool", e)

        @block.sync
        def _(e):
            run("sp", e)
    es.close()
    return nc


def _host_inputs(inp):
    f32 = np.float32
    x = np.asarray(inp["x"], f32)
    w_in = np.asarray(inp["w_in"], f32)[0]
    b_in = np.asarray(inp["b_in"], f32)[0]

    def pcol(v):
        return np.ascontiguousarray(np.asarray(v, f32).reshape(-1, 128).T)

    cols = []
    zero_col = -1
    for j in range(4):
        cols += list(range(j * 64, j * 64 + 64)) + list(range((4 + j) * 64, (4 + j) * 64 + 64))
    cols += list(range(512, 640)) + list(range(640, 768)) + list(range(768, 1280))

    def partner(base):
        c = [zero_col] * 64
        for i in range(8):
            c[i] = base + 8 + i
            c[8 + i] = base + i
        return c
    for j in range(4):
        cols += partner(j * 64) + partner((4 + j) * 64)
    cols += partner(512) + partner(576)
    cols = np.array(cols)
    w_ext = np.concatenate([w_in, np.zeros((D, 1), f32)], axis=1)
    b_ext = np.concatenate([b_in, np.zeros((1,), f32)])
    win = np.ascontiguousarray(w_ext[:, cols].reshape(8, 128, 15, 128).transpose(2, 1, 0, 3).reshape(15, 128, 1024))
    ball = pcol(b_ext[cols])
    shared = dict(
        win=win, ball=ball, g1=pcol(inp["ln1_g"][0]),
        bvb=np.ascontiguousarray(np.broadcast_to(b_in[640:768][None, :], (128, 128))).astype(f32),
        identb=np.eye(128, dtype=f32).astype(ml_dtypes.bfloat16), identf=np.eye(128, dtype=f32),
        swapj=np.roll(np.eye(128, dtype=f32), 64, axis=1), onesm=np.ones((128, 128), f32),
    )
    sinks = np.asarray(inp["sinks"], f32)[0]
    shared["sinkb"] = np.ascontiguousarray(np.broadcast_to(sinks[None, :], (128, 8))).astype(f32)
    lre = np.asarray(inp["lam_re"], f32)[0].T
    lim = np.asarray(inp["lam_im"], f32)[0].T
    shared["lre"] = np.ascontiguousarray(np.concatenate([lre, lre], 0))
    shared["lim"] = np.ascontiguousarray(np.concatenate([lim, lim], 0))
    shared["lstep"] = np.ascontiguousarray(np.broadcast_to(np.asarray(inp["log_step"], f32)[0][None, :], (128, G))).astype(f32)
    shared["sgn"] = np.concatenate([-np.ones((64, 1), f32), np.ones((64, 1), f32)], 0)
    bre = np.asarray(inp["ssm_b_re"], f32)[0].transpose(1, 0, 2).reshape(64, G * 16)
    bim = np.asarray(inp["ssm_b_im"], f32)[0].transpose(1, 0, 2).reshape(64, G * 16)
    shared["bs1"] = np.ascontiguousarray(np.concatenate([bre, bim], 0))
    shared["bs2"] = np.ascontiguousarray(np.concatenate([bim, bre], 0))
    cre = np.asarray(inp["ssm_c_re"], f32)[0].reshape(4, 128, 64)
    cim = np.asarray(inp["ssm_c_im"], f32)[0].reshape(4, 128, 64)
    shared["csa"] = np.ascontiguousarray(np.concatenate([cre, cim], 2).transpose(1, 0, 2).reshape(128, 512))
    shared["csb"] = np.ascontiguousarray(np.concatenate([cim, cre], 2).transpose(1, 0, 2).reshape(128, 512))
    shared["dsk"] = pcol(np.asarray(inp["ssm_d"], f32)[0].reshape(-1))
    shared["tcol"] = np.ascontiguousarray(np.broadcast_to(np.arange(1, QT + 1, dtype=f32)[None, :], (128, QT))).astype(f32)
    shared["wglu"] = np.ascontiguousarray(np.asarray(inp["w_glu"], f32)[0])
    shared["bglu"] = pcol(inp["b_glu"][0])
    shared["wout"] = np.ascontiguousarray(np.asarray(inp["w_out"], f32)[0])
    shared["gmix"] = pcol(np.concatenate([np.asarray(inp["g_attn"], f32)[0], np.asarray(inp["g_ssm"], f32)[0]]))
    shared["g2"] = pcol(inp["ln2_g"][0])
    wu = np.asarray(inp["w_up"], f32)[0].reshape(8, 128, 2, NFC, 128)
    shared["wup"] = np.ascontiguousarray(wu.transpose(3, 1, 0, 2, 4).reshape(NFC, 128, 2048))
    cwv = np.asarray(inp["conv_w"], f32)[0]
    shared["cw"] = np.ascontiguousarray(cwv.T.reshape(44, 128, 3).transpose(1, 0, 2).reshape(128, 132))
    shared["cb"] = pcol(inp["conv_b"][0])
    shared["wdown"] = np.ascontiguousarray(np.asarray(inp["w_down"], f32)[0])
    shared["lnf"] = np.ascontiguousarray(np.broadcast_to(np.asarray(inp["lnf_g"], f32)[None, :], (128, D))).astype(f32)

    qi = np.arange(128)[:, None]
    kj = np.arange(256)[None, :]
    diff = qi + 128 - kj
    band = (diff >= 0) & (diff < 128)
    m_gen = np.where(band, 0.0, -1e30).astype(f32)
    m_first = np.where(band & (kj >= 128), 0.0, -1e30).astype(f32)
    inv_freq = (500000.0 ** (-np.arange(8, dtype=np.float64) * 2.0 / 16.0))
    maps = []
    for c in range(8):
        b, q = divmod(c, 4)
        xwv = np.zeros((4, QT, D), f32)
        fl = np.zeros((128, 4), f32)
        for k in range(4):
            qq = q - 3 + k
            if qq >= 0:
                xwv[k] = x[b, qq * QT:(qq + 1) * QT]
                fl[:, k] = 1.0
        pos = (q * QT - 512 + np.arange(NPT)).astype(np.float64)
        ang = (pos[None, :].astype(np.float32) * inv_freq[:, None].astype(np.float32)).astype(np.float64)
        cs, sn = np.cos(ang).astype(f32), np.sin(ang).astype(f32)
        rc = np.ones((128, NPT), f32)
        rs = np.zeros((128, NPT), f32)
        for b0 in (0, 64):
            rc[b0:b0 + 8] = cs
            rc[b0 + 8:b0 + 16] = cs
            rs[b0:b0 + 8] = -sn
            rs[b0 + 8:b0 + 16] = sn
        m = dict(shared)
        m.update(xw=xwv, flags=fl, ropec=rc, ropes=rs,
                 mask=np.ascontiguousarray(np.concatenate([m_first if q == 0 else m_gen, m_gen], 1)))
        maps.append(m)
    return maps


_NC_CACHE = {}
DBG_INFO = {}


def kernel(**inputs):
    maps = _host_inputs(inputs)
    if "nc" not in _NC_CACHE:
        st = os.environ.get("KSTOP")
        dm = os.environ.get("KDUMP", "0,16384").split(",")
        _NC_CACHE["nc"] = build_nc(stop=st, dump=(int(dm[0]), int(dm[1])))
    nc = _NC_CACHE["nc"]
    res = run_bass_kernel_spmd(nc, maps, core_ids=list(range(8)))
    out = np.zeros((2, SEQ, D), np.float32)
    for c in range(8):
        b, q = divmod(c, 4)
        out[b, q * QT:(q + 1) * QT] = np.asarray(res.results[c]["out"], np.float32)
    return out
```

```python
import math
import os
from contextlib import ExitStack

import numpy as np
import ml_dtypes

import concourse.bass as bass
import concourse.mybir as mybir
from concourse.bass_utils import run_bass_kernel_spmd

F32 = mybir.dt.float32
BF = mybir.dt.bfloat16
AF = mybir.ActivationFunctionType
ALU = mybir.AluOpType
AX = mybir.AxisListType

D = 1024
SEQ = 8192
QT = 2048
NH = 8
G = 32
DFF = 2816
NFC = DFF // 128
EPS = 1e-5
PI = math.pi
TWO_PI = 2.0 * math.pi
NW = 53000
NTOK = QT + 128
NPT = 2560

ENGS = ["pe", "act", "dve", "pool", "sp"]


class Sch:
    def __init__(self):
        self.q = {e: [] for e in ENGS}
        self.cnt = {e: 0 for e in ENGS}
        self.seen = {e: {} for e in ENGS}
        self.lw = {}
        self.rd = {}
        self.dcnt = {}

    def _deps(self, eng, reads, writes):
        need = {}

        def add(ev):
            h, v = ev
            if need.get(h, 0) < v:
                need[h] = v

        for b in list(reads) + list(writes):
            if b in self.lw:
                add(self.lw[b])
        for b in writes:
            for h, v in self.rd.get(b, {}).items():
                add((h, v))
        out = []
        for h, v in need.items():
            if self.seen[eng].get(h, 0) < v:
                self.seen[eng][h] = v
                out.append((h, v))
        return out

    def _commit(self, ev, reads, writes):
        for b in writes:
            self.lw[b] = ev
            self.rd[b] = {}
        for b in reads:
            d = self.rd.setdefault(b, {})
            if d.get(ev[0], 0) < ev[1]:
                d[ev[0]] = ev[1]

    def op(self, eng, fn, reads=(), writes=()):
        waits = self._deps(eng, reads, writes)
        self.cnt[eng] += 1
        ev = (eng, self.cnt[eng])
        self.q[eng].append((waits, fn, eng))
        self._commit(ev, reads, writes)

    def dma(self, eng, out_ap, in_ap, reads=(), writes=(), key=None):
        waits = self._deps(eng, reads, writes)
        k = ("d", key or writes[0])
        self.dcnt[k] = self.dcnt.get(k, 0) + 16
        ev = (k, self.dcnt[k])
        self.q[eng].append((waits, lambda e, o=out_ap, i=in_ap: e.dma_start(out=o, in_=i), k))
        self._commit(ev, reads, writes)

    def fence(self):
        for a in ENGS:
            waits = []
            for e in ENGS:
                if e != a and self.cnt[e] > self.seen[a].get(e, 0):
                    self.seen[a][e] = self.cnt[e]
                    waits.append((e, self.cnt[e]))
            for k, v in self.dcnt.items():
                if v > self.seen[a].get(k, 0):
                    self.seen[a][k] = v
                    waits.append((k, v))
            self.q[a].append((waits, None, None))


class Arena:
    def __init__(self, t):
        self.t = t
        self.top = 0

    def f32(self, n):
        a = self.t[:, self.top:self.top + n]
        self.top += n
        assert self.top <= NW, self.top
        return a

    def bf(self, n):
        w = (n + 1) // 2
        a = self.t[:, self.top:self.top + w].bitcast(BF)
        self.top += w
        assert self.top <= NW, self.top
        return a[:, 0:n]


class _Stop(Exception):
    pass


def build_nc(stop=None, dump=(0, 16384)):
    nc = bass.Bass("TRN2", target_bir_lowering=False)
    S = Sch()

    def checkpoint(name):
        if stop == name:
            S.fence()
            ov = out_d.rearrange("(p a) n -> p (a n)", p=128)
            n = dump[1] - dump[0]
            S.dma("sp", ov[:, 0:n], arena_t[:, dump[0]:dump[1]], writes=["dump"], key="dump")
            S.fence()
            raise _Stop()

    def din(name, shape, dt=F32):
        return nc.dram_tensor(name, list(shape), dt, kind="ExternalInput").ap()

    xw = din("xw", [4, QT, D])
    win = din("win", [15, 128, 8 * 128])
    ball_d = din("ball", [128, 15])
    g1_d = din("g1", [128, 8])
    bvb_d = din("bvb", [128, 128])
    ropec_d = din("ropec", [128, NPT])
    ropes_d = din("ropes", [128, NPT])
    mask_d = din("mask", [128, 512])
    sink_d = din("sinkb", [128, 8])
    flags_d = din("flags", [128, 4])
    identb_d = din("identb", [128, 128], BF)
    identf_d = din("identf", [128, 128])
    swapj_d = din("swapj", [128, 128])
    ones_d = din("onesm", [128, 128])
    lre_d = din("lre", [128, G])
    lim_d = din("lim", [128, G])
    lstep_d = din("lstep", [128, G])
    sgn_d = din("sgn", [128, 1])
    bs1_d = din("bs1", [128, G * 16])
    bs2_d = din("bs2", [128, G * 16])
    csa_d = din("csa", [128, 4 * 128])
    csb_d = din("csb", [128, 4 * 128])
    dsk_d = din("dsk", [128, 4])
    tcol_d = din("tcol", [128, QT])
    wglu_d = din("wglu", [512, 512])
    bglu_d = din("bglu", [128, 4])
    wout_d = din("wout", [D, D])
    gmix_d = din("gmix", [128, 8])
    g2_d = din("g2", [128, 8])
    wup_d = din("wup", [NFC, 128, 8 * 256])
    cw_d = din("cw", [128, 44 * 3])
    cb_d = din("cb", [128, 44])
    wdown_d = din("wdown", [DFF, D])
    lnf_d = din("lnf", [128, D])
    out_d = nc.dram_tensor("out", [QT, D], F32, kind="ExternalOutput").ap()
    hs_d = nc.dram_tensor("hs", [NTOK, D], F32, kind="Internal").ap()

    es = ExitStack()
    arena_t = es.enter_context(nc.sbuf_tensor("arena", [128, NW], F32))
    AR = Arena(arena_t)
    ps = [es.enter_context(nc.psum_tensor(f"ps{i}", [128, 512], F32)) for i in range(8)]
    psk = [f"ps{i}" for i in range(8)]
    esem = {e: es.enter_context(nc.semaphore(f"sem_{e}")) for e in ENGS}

    ident_b = AR.bf(128)
    ident_f = AR.f32(128)
    swapj = AR.f32(128)
    onesm = AR.f32(128)
    flags = AR.f32(4)
    maskt = AR.f32(512)
    sinkb = AR.f32(8)
    ball = AR.f32(15)
    g1 = AR.f32(8)
    bvb = AR.f32(128)
    sgn = AR.f32(1)
    dsk = AR.f32(4)
    bglu = AR.f32(4)
    gmix = AR.f32(8)
    g2 = AR.f32(8)
    cw = AR.f32(132)
    cb = AR.f32(44)
    negpi = AR.f32(1)
    for dst, src in [(ident_b, identb_d), (ident_f, identf_d), (swapj, swapj_d), (onesm, ones_d),
                     (flags, flags_d), (maskt, mask_d), (sinkb, sink_d), (ball, ball_d), (g1, g1_d),
                     (bvb, bvb_d), (sgn, sgn_d), (dsk, dsk_d), (bglu, bglu_d), (gmix, gmix_d),
                     (g2, g2_d), (cw, cw_d), (cb, cb_d)]:
        S.dma("sp", dst, src, writes=["consts"], key="setup")
    halfpi = AR.f32(1)
    S.op("dve", lambda e: e.memset(negpi, -PI), writes=["negpi"])
    S.op("dve", lambda e: e.memset(halfpi, 0.5 * PI), writes=["negpi"])

    def rsqrt_ops(X, key):
        S.op("act", lambda e: e.activation(X, X, AF.Sqrt), reads=[key], writes=[key])
        S.op("dve", lambda e: e.reciprocal(X, X), reads=[key], writes=[key])

    I32 = mybir.dt.int32

    def trig_ops(phi, ki, ab, out_sin, out_cos, kphi, kki, kab, ksin, kcos, extra_reads=()):
        S.op("dve", lambda e: e.tensor_scalar(ki, phi, 1.0 / TWO_PI, None, ALU.mult),
             reads=[kphi] + list(extra_reads), writes=[kki])
        S.op("dve", lambda e: e.scalar_tensor_tensor(phi, ki, -TWO_PI, phi, ALU.mult, ALU.add),
             reads=[kki, kphi], writes=[kphi])
        S.op("dve", lambda e: e.tensor_scalar(phi, phi, PI, -PI, ALU.min, ALU.max), reads=[kphi], writes=[kphi])
        S.op("act", lambda e: e.activation(out_sin, phi, AF.Sin), reads=[kphi], writes=[ksin])
        S.op("act", lambda e: e.activation(ab, phi, AF.Abs), reads=[kphi, kki], writes=[kab])
        S.op("act", lambda e: e.activation(out_cos, ab, AF.Sin, bias=halfpi, scale=-1.0), reads=[kab, "negpi"], writes=[kcos])
    R_MIX = AR.top
    mixT = AR.bf(8 * NTOK).rearrange("p (c t) -> p c t", c=8)
    persist_top = AR.top

    try:
        winb = AR.bf(8 * 1920).rearrange("p (c n) -> p c n", c=8)
        uT = AR.bf(4 * 4 * QT).rearrange("p (c k t) -> p c k t", c=4, k=4)
        qTt = AR.bf(4 * NPT).rearrange("p (c t) -> p c t", c=4)
        kTt = AR.bf(NPT)
        Vt = AR.bf(20 * 128).rearrange("p (j n) -> p j n", j=20)
        phaseB_top = AR.top
        R_A = persist_top
        R_U = R_A + 8 * 1920 // 2
        R_Q = R_U + 4 * 4 * QT // 2
        wst = [AR.f32(1024).rearrange("p (c n) -> p c n", c=8) for _ in range(2)]
        AR.top = phaseB_top
        hnT = [AR.bf(8 * 512).rearrange("p (c t) -> p c t", c=8) for _ in range(2)]
        xt = [AR.f32(1024) for _ in range(3)]
        hnb = [AR.bf(1024) for _ in range(3)]
        sst = [AR.f32(2) for _ in range(3)]
        rct = [AR.f32(512)]
        rst = [AR.f32(512)]
        tA = AR.f32(512)
        tB = AR.f32(512)

        checkpoint("A0")
        for ch in range(15):
            sl = ch % 2
            S.dma("sp", wst[sl].rearrange("p c n -> p (c n)"), win[ch], writes=[f"wst{sl}"])
            S.op("pool" if ch % 2 == 0 else "dve", lambda e, sl=sl, ch=ch: e.tensor_tensor(
                winb[:, :, ch * 128:(ch + 1) * 128], wst[sl],
                g1.unsqueeze(2).to_broadcast([128, 8, 128]), ALU.mult),
                reads=[f"wst{sl}", "consts"], writes=[f"winb{ch}"])
        S.fence()
        checkpoint("A1")

        def rms_tile(src_dram, sl, sskey):
            S.dma("sp", xt[sl], src_dram, writes=[f"xt{sl}"])
            S.op("dve", lambda e, sl=sl: e.scalar_tensor_tensor(hnb[sl], xt[sl], 1.0, xt[sl], ALU.mult, ALU.mult,
                                                               accum_out=sst[sl][:, 0:1]),
                 reads=[f"xt{sl}"], writes=[f"hnb{sl}", f"ss{sl}"])
            S.op("dve", lambda e, sl=sl: e.tensor_scalar(sst[sl][:, 1:2], sst[sl][:, 0:1], 1.0 / D, EPS, ALU.mult, ALU.add),
                 reads=[f"ss{sl}"], writes=[f"rs{sl}"])
            rsqrt_ops(sst[sl][:, 1:2], f"rs{sl}")
            S.op("dve", lambda e, sl=sl: e.tensor_scalar(hnb[sl], xt[sl], sst[sl][:, 1:2], None, ALU.mult),
                 reads=[f"xt{sl}", f"rs{sl}"], writes=[f"hnb{sl}"])

        def transpose8(src_bf, pbank, dst_fn, rkeys, wkeys, nchunks=8):
            pv = ps[pbank][:].bitcast(BF)

            def f(e):
                ins = None
                for c in range(nchunks):
                    ins = e.transpose(pv[:, c * 128:(c + 1) * 128], src_bf[:, c * 128:(c + 1) * 128], ident_b)
                return ins
            S.op("pe", f, reads=list(rkeys) + ["consts"], writes=[psk[pbank]])
            S.op("act", lambda e: e.activation(dst_fn(), pv[:, 0:nchunks * 128].rearrange("p (c t) -> p c t", c=nchunks), AF.Copy),
                 reads=[psk[pbank]], writes=list(wkeys))

        pbank_rr = [0]

        def next_bank(lo, hi):
            b = lo + pbank_rr[0] % (hi - lo)
            pbank_rr[0] += 1
            return b

        tile_ctr = 0
        fidx = 0
        for k in range(4):
            for tc in range(4):
                cs = (k * 4 + tc) % 2
                full = (k == 3) or (k == 2 and tc == 3)
                for j in range(4):
                    sl = tile_ctr % 3
                    pbk = 6 + tile_ctr % 2
                    if tile_ctr == 0:
                        rms_tile(xw[0, 0:128, :], 0, None)
                    nxt = tile_ctr + 1
                    if nxt < 64:
                        rms_tile(xw[nxt // 16, (nxt % 16) * 128:(nxt % 16 + 1) * 128, :], nxt % 3, None)
                    tile_ctr += 1
                    transpose8(hnb[sl], pbk,
                               lambda cs=cs, j=j: hnT[cs][:, :, j * 128:(j + 1) * 128],
                               [f"hnb{sl}"], [f"hnT{cs}_{j}"])
                hkeys = [f"hnT{cs}_{j}" for j in range(4)]
                if k == 0 and tc == 0:
                    checkpoint("A2")
                for cj in range(4):
                    pb = next_bank(0, 6)

                    def f(e, cj=cj, pb=pb, cs=cs):
                        ins = None
                        for dc in range(8):
                            ins = e.matmul(ps[pb][:], winb[:, dc, (6 + cj) * 128:(7 + cj) * 128], hnT[cs][:, dc, :],
                                           start=(dc == 0), stop=(dc == 7))
                        return ins
                    S.op("pe", f, reads=hkeys + [f"winb{6 + cj}"], writes=[psk[pb]])
                    S.op("act", lambda e, cj=cj, pb=pb, k=k, tc=tc: e.activation(
                        uT[:, cj, k, tc * 512:(tc + 1) * 512], ps[pb][:], AF.Identity, bias=ball[:, 6 + cj:7 + cj]),
                        reads=[psk[pb], "consts"], writes=[f"uT{cj}_{k}"])
                if k == 0 and tc == 0:
                    checkpoint("A3")
                if k == 2 and tc == 3:
                    checkpoint("A3b")
                if not full:
                    continue
                c0 = fidx * 512
                rsl = 0
                S.dma("sp", rct[rsl], ropec_d[:, c0:c0 + 512], writes=[f"rc{rsl}"])
                S.dma("sp", rst[rsl], ropes_d[:, c0:c0 + 512], writes=[f"rsn{rsl}"])
                if fidx == 0:
                    checkpoint("A3c")
                for cj in range(5):
                    pm = next_bank(0, 6)
                    pp = next_bank(0, 6)

                    def fm(e, ch=cj, pb=pm, cs=cs):
                        ins = None
                        for dc in range(8):
                            ins = e.matmul(ps[pb][:], winb[:, dc, ch * 128:(ch + 1) * 128], hnT[cs][:, dc, :],
                                           start=(dc == 0), stop=(dc == 7))
                        return ins
                    S.op("pe", fm, reads=hkeys + [f"winb{cj}"], writes=[psk[pm]])
                    if fidx == 0 and cj == 0:
                        checkpoint("A4m")
                    S.op("pe", lambda e, ch=10 + cj, pb=pp, cs=cs: [e.matmul(
                        ps[pb][:], winb[:, dc, ch * 128:(ch + 1) * 128], hnT[cs][:, dc, :],
                        start=(dc == 0), stop=(dc == 7)) for dc in range(8)][-1],
                        reads=hkeys + [f"winb{10 + cj}"], writes=[psk[pp]])
                    if fidx == 0 and cj == 0:
                        checkpoint("A4p")
                    dst = qTt[:, cj, c0:c0 + 512] if cj < 4 else kTt[:, c0:c0 + 512]
                    dkey = f"qk{cj}_{fidx}"
                    S.op("dve", lambda e, pb=pm, cj=cj, rsl=rsl: e.scalar_tensor_tensor(
                        tA, ps[pb][:], ball[:, cj:cj + 1], rct[rsl], ALU.add, ALU.mult),
                        reads=[psk[pm], f"rc{rsl}", "consts"], writes=["tA"])
                    S.op("dve", lambda e, pb=pp, cj=cj, rsl=rsl: e.scalar_tensor_tensor(
                        tB, ps[pb][:], ball[:, 10 + cj:11 + cj], rst[rsl], ALU.add, ALU.mult),
                        reads=[psk[pp], f"rsn{rsl}", "consts"], writes=["tB"])
                    S.op("pool", lambda e, dst=dst: e.tensor_tensor(dst, tA, tB, ALU.add),
                         reads=["tA", "tB"], writes=[dkey])
                if fidx == 0:
                    checkpoint("A4c")
                for j in range(4):
                    pb = next_bank(0, 6)
                    S.op("pe", lambda e, pb=pb, cs=cs, j=j: [e.matmul(
                        ps[pb][:, 0:128], hnT[cs][:, dc, j * 128:(j + 1) * 128], winb[:, dc, 5 * 128:6 * 128],
                        start=(dc == 0), stop=(dc == 7)) for dc in range(8)][-1],
                        reads=hkeys + ["winb5"], writes=[psk[pb]])
                    S.op("dve", lambda e, pb=pb, jj=fidx * 4 + j: e.tensor_tensor(
                        Vt[:, jj, :], ps[pb][:, 0:128], bvb, ALU.add),
                        reads=[psk[pb], "consts"], writes=[f"V{fidx * 4 + j}"])
                fidx += 1
                if fidx == 1:
                    checkpoint("A4")

        checkpoint("A")
        S.fence()
        AR.top = phaseB_top
        ssb = AR.f32(8 * 256).rearrange("p (h k) -> p h k", h=8)
        pbf = AR.bf(8 * 256).rearrange("p (h k) -> p h k", h=8)
        ptb = AR.bf(16 * 128).rearrange("p (j q) -> p j q", j=16)
        att = AR.f32(512)
        attb = AR.bf(512)
        sm8 = AR.f32(64)
        mx, negm, rsum, esk, den, rden = (sm8[:, i * 8:(i + 1) * 8] for i in range(6))
        ass = sm8[:, 48:49]
        arstd = sm8[:, 49:50]
        pbf2 = AR.bf(8 * 256).rearrange("p (h k) -> p h k", h=8)
        pbfs = [pbf, pbf2]
        rdens = [rden, sm8[:, 56:64]]

        def att_stage1(n):
            b = n % 2
            jt = n + 3
            kc0 = (jt - 1) * 128
            mk = maskt[:, 0:256] if n == 1 else maskt[:, 256:512]
            for h in range(8):
                bank = h // 2
                b0 = 64 * (h // 4)
                S.op("pe", lambda e, h=h, bank=bank, b0=b0: e.matmul(
                    ps[bank][:, (h % 2) * 256:(h % 2) * 256 + 256],
                    qTt[b0:b0 + 64, h % 4, jt * 128:(jt + 1) * 128], kTt[b0:b0 + 64, kc0:kc0 + 256],
                    start=True, stop=True),
                    reads=[f"qk{h % 4}_{jt // 4}", f"qk4_{(jt - 1) // 4}", f"qk4_{jt // 4}"], writes=[psk[bank]])
            for bank in range(4):
                S.op("dve", lambda e, bank=bank: e.scalar_tensor_tensor(
                    ssb[:, 2 * bank:2 * bank + 2, :], ps[bank][:].rearrange("p (h k) -> p h k", h=2), 0.125,
                    mk.unsqueeze(1).to_broadcast([128, 2, 256]), ALU.mult, ALU.add),
                    reads=[psk[bank], "consts"], writes=[f"ssb{bank}"])
            skeys = [f"ssb{i}" for i in range(4)]
            S.op("dve", lambda e: e.reduce_max(mx, ssb, AX.X), reads=skeys, writes=["mx"])
            S.op("dve", lambda e: e.tensor_tensor(negm, mx, sinkb, ALU.max), reads=["mx", "consts"], writes=["negm"])
            S.op("dve", lambda e: e.tensor_scalar(negm, negm, -1.0, None, ALU.mult), reads=["negm"], writes=["negm"])
            for h in range(8):
                S.op("act", lambda e, h=h: e.activation(pbfs[b][:, h, :], ssb[:, h, :], AF.Exp, bias=negm[:, h:h + 1],
                                                        accum_out=rsum[:, h:h + 1]),
                     reads=[f"ssb{h // 2}", "negm"], writes=[f"pbf{b}_{h}", f"rsum{h}"])
            S.op("dve", lambda e: e.tensor_tensor(esk, sinkb, negm, ALU.add), reads=["negm", "consts"], writes=["esk"])
            S.op("act", lambda e: e.activation(esk, esk, AF.Exp), reads=["esk"], writes=["esk"])
            S.op("dve", lambda e: e.tensor_tensor(den, rsum, esk, ALU.add),
                 reads=["esk"] + [f"rsum{h}" for h in range(8)], writes=["den"])
            S.op("dve", lambda e: e.reciprocal(rdens[b], den), reads=["den"], writes=[f"rden{b}"])

        def att_stage2(n):
            b = n % 2
            jt = n + 3
            for half in range(2):
                pv = ps[4 + half][:].bitcast(BF)

                def f(e, half=half, pv=pv):
                    ins = None
                    for i in range(8):
                        h = half * 4 + i // 2
                        kk = i % 2
                        ins = e.transpose(pv[:, i * 128:(i + 1) * 128], pbfs[b][:, h, kk * 128:(kk + 1) * 128], ident_b)
                    return ins
                S.op("pe", f, reads=[f"pbf{b}_{half * 4 + i}" for i in range(4)] + ["consts"], writes=[psk[4 + half]])
                if half == 0:
                    S.op("act", lambda e, pv=pv: e.activation(
                        ptb[:, 0:8, :], pv.rearrange("p (j q) -> p j q", j=8), AF.Copy),
                        reads=[psk[4]], writes=["ptb0"])
                else:
                    S.op("dve", lambda e, pv=pv: e.tensor_copy(
                        ptb[:, 8:16, :], pv.rearrange("p (j q) -> p j q", j=8)),
                        reads=[psk[5]], writes=["ptb1"])

            def fpv(e):
                ins = None
                for h in range(8):
                    kv = h // 4
                    for kk in range(2):
                        ins = e.matmul(ps[6][:, h * 64:(h + 1) * 64], ptb[:, h * 2 + kk, :],
                                       Vt[:, jt - 1 + kk, kv * 64:(kv + 1) * 64], start=(kk == 0), stop=(kk == 1))
                return ins
            S.op("pe", fpv, reads=["ptb0", "ptb1", f"V{jt - 1}", f"V{jt}"], writes=[psk[6]])
            S.op("dve", lambda e: e.tensor_tensor(
                att.rearrange("p (h d) -> p h d", h=8), ps[6][:].rearrange("p (h d) -> p h d", h=8),
                rdens[b].unsqueeze(2).to_broadcast([128, 8, 64]), ALU.mult),
                reads=[psk[6], f"rden{b}"], writes=["att"])
            S.op("dve", lambda e: e.scalar_tensor_tensor(attb, att, 1.0, att, ALU.mult, ALU.mult, accum_out=ass),
                 reads=["att"], writes=["attb", "ass"])
            S.op("dve", lambda e: e.tensor_scalar(arstd, ass, 1.0 / 512, EPS, ALU.mult, ALU.add), reads=["ass"], writes=["arstd"])
            rsqrt_ops(arstd, "arstd")
            S.op("dve", lambda e: e.tensor_scalar(attb, att, arstd, None, ALU.mult), reads=["att", "arstd"], writes=["attb"])
            transpose8(attb, 7, lambda: mixT[:, 0:4, n * 128:(n + 1) * 128], ["attb"], [f"mixA{n}"], nchunks=4)

        att_stage1(0)
        for n in range(17):
            if n + 1 < 17:
                att_stage1(n + 1)
            att_stage2(n)

        checkpoint("B")
        S.fence()
        AR.top = R_A
        LX = AR.bf(33 * 128).rearrange("p (g n) -> p g n", g=33)
        LXp = AR.bf(33 * 128).rearrange("p (g n) -> p g n", g=33)
        LY1 = AR.bf(33 * 128).rearrange("p (g n) -> p g n", g=33)
        assert AR.top <= R_U
        AR.top = R_Q
        ssm_top = AR.top
        LY2 = AR.bf(33 * 128).rearrange("p (g n) -> p g n", g=33)
        tcol = AR.f32(QT)
        pr = {nm: AR.f32(G) for nm in ["lre", "lim", "dt", "a", "th", "rr", "t1", "t2", "lbre", "lbim", "nr",
                                       "den", "cre", "cim", "cimS", "creN", "thm", "fA", "fB", "ki", "ab", "negA", "a2048", "R", "R1536", "Rm512"]}
        fAk = [AR.f32(G) for _ in range(3)]
        fBk = [AR.f32(G) for _ in range(3)]
        carry = AR.f32(2)
        ctmp = AR.f32(2)
        zacc = AR.f32(8)
        zt = AR.f32(2)
        zt2 = AR.f32(2)
        loop_top = AR.top
        C1 = AR.f32(QT)
        S1 = AR.f32(QT)
        bq = AR.f32(QT)
        wv = AR.f32(QT)
        p12_off = AR.top
        P1 = AR.bf(QT)
        P2 = AR.bf(QT)
        psc = arena_t[:, p12_off:p12_off + QT]
        tm_off = AR.top
        tm1 = [AR.f32(512) for _ in range(2)]
        tm2 = [AR.f32(512) for _ in range(2)]
        tmall = arena_t[:, tm_off:tm_off + QT]
        TMK = ["tm10", "tm11", "tm20", "tm21"]
        AR.top = loop_top
        bs1 = AR.f32(G * 16).rearrange("p (g c) -> p g c", g=G)
        bs2 = AR.f32(G * 16).rearrange("p (g c) -> p g c", g=G)
        bt = AR.f32(G * 16).rearrange("p (g c) -> p g c", g=G)
        bt2 = AR.f32(G * 16).rearrange("p (g c) -> p g c", g=G)
        bz = AR.f32(33 * 128)
        bzp = AR.f32(33 * 128)
        csa = AR.f32(512)
        csb = AR.f32(512)
        lyd = AR.f32(512)

        for dst, src in [(pr["lre"], lre_d), (pr["lim"], lim_d), (pr["dt"], lstep_d), (tcol, tcol_d),
                         (bs1.rearrange("p g c -> p (g c)"), bs1_d), (bs2.rearrange("p g c -> p (g c)"), bs2_d),
                         (csa, csa_d), (csb, csb_d)]:
            S.dma("sp", dst, src, writes=["ssmc"], key="setup2")

        def dv(fn, r=("ssmp",), w=("ssmp",), eng="dve"):
            S.op(eng, fn, reads=list(r) + ["ssmc", "consts", "negpi"], writes=list(w))

        p = pr
        dv(lambda e: e.activation(p["dt"], p["dt"], AF.Exp), eng="act")
        dv(lambda e: e.tensor_tensor(p["a"], p["lre"], p["dt"], ALU.mult))
        dv(lambda e: e.tensor_tensor(p["th"], p["lim"], p["dt"], ALU.mult))
        dv(lambda e: e.activation(p["rr"], p["a"], AF.Exp), eng="act")
        dv(lambda e: e.activation(p["R"], p["a"], AF.Exp, scale=float(QT)), eng="act")
        dv(lambda e: e.activation(p["R1536"], p["a"], AF.Exp, scale=1536.0), eng="act")
        dv(lambda e: e.activation(p["Rm512"], p["a"], AF.Exp, scale=-512.0), eng="act")
        dv(lambda e: e.tensor_scalar(p["negA"], p["a"], -1.0, None, ALU.mult))
        dv(lambda e: e.tensor_scalar(p["a2048"], p["a"], float(QT), None, ALU.mult))
        dv(lambda e: e.memset(zt, 0.0))
        dv(lambda e: e.memset(zt2, 0.0))
        dv(lambda e: e.tensor_copy(p["thm"], p["th"]))
        trig_ops(p["thm"], p["ki"].bitcast(I32), p["ab"], p["t1"], p["t2"], "ssmp", "ssmp", "ssmp", "ssmp", "ssmp",
                 extra_reads=["ssmc", "consts", "negpi"])
        dv(lambda e: e.tensor_tensor(p["lbre"], p["rr"], p["t2"], ALU.mult))
        dv(lambda e: e.tensor_tensor(p["lbim"], p["rr"], p["t1"], ALU.mult))
        dv(lambda e: e.tensor_scalar(p["nr"], p["lbre"], -1.0, None, ALU.add))
        dv(lambda e: e.tensor_tensor(p["den"], p["lre"], p["lre"], ALU.mult))
        dv(lambda e: e.tensor_tensor(p["t1"], p["lim"], p["lim"], ALU.mult))
        dv(lambda e: e.tensor_tensor(p["den"], p["den"], p["t1"], ALU.add))
        dv(lambda e: e.reciprocal(p["den"], p["den"]))
        dv(lambda e: e.tensor_tensor(p["cre"], p["nr"], p["lre"], ALU.mult))
        dv(lambda e: e.tensor_tensor(p["t1"], p["lbim"], p["lim"], ALU.mult))
        dv(lambda e: e.tensor_tensor(p["cre"], p["cre"], p["t1"], ALU.add))
        dv(lambda e: e.tensor_tensor(p["cre"], p["cre"], p["den"], ALU.mult))
        dv(lambda e: e.tensor_tensor(p["cim"], p["lbim"], p["lre"], ALU.mult))
        dv(lambda e: e.tensor_tensor(p["t1"], p["nr"], p["lim"], ALU.mult))
        dv(lambda e: e.tensor_tensor(p["cim"], p["cim"], p["t1"], ALU.subtract))
        dv(lambda e: e.tensor_tensor(p["cim"], p["cim"], p["den"], ALU.mult))
        dv(lambda e: e.tensor_scalar(p["cimS"], p["cim"], sgn[:, 0:1], None, ALU.mult))
        dv(lambda e: e.tensor_scalar(p["creN"], p["cre"], sgn[:, 0:1], -1.0, ALU.mult, ALU.mult))
        dv(lambda e: e.tensor_scalar(p["t1"], p["thm"], float(QT), None, ALU.mult))
        trig_ops(p["t1"], p["ki"].bitcast(I32), p["ab"], p["fB"], p["fA"], "ssmp", "ssmp", "ssmp", "ssmp", "ssmp",
                 extra_reads=["ssmc", "consts", "negpi"])
        dv(lambda e: e.tensor_scalar(p["fB"], p["fB"], sgn[:, 0:1], None, ALU.mult))
        for k in range(3):
            dv(lambda e, k=k: e.tensor_scalar(fAk[k], p["fA"], flags[:, k:k + 1], None, ALU.mult))
            dv(lambda e, k=k: e.tensor_scalar(fBk[k], p["fB"], flags[:, k:k + 1], None, ALU.mult))
        bc = lambda t: t.unsqueeze(2).to_broadcast([128, G, 16])
        dv(lambda e: e.tensor_tensor(bt, bs1, bc(p["cre"]), ALU.mult))
        dv(lambda e: e.tensor_tensor(bt2, bs2, bc(p["cimS"]), ALU.mult))
        dv(lambda e: e.tensor_tensor(bt, bt, bt2, ALU.add))
        dv(lambda e: e.tensor_tensor(bt2, bs2, bc(p["creN"]), ALU.mult))
        dv(lambda e: e.tensor_tensor(bs1, bs1, bc(p["cim"]), ALU.mult))
        dv(lambda e: e.tensor_tensor(bt2, bt2, bs1, ALU.add))
        dv(lambda e: e.memset(bz, 0.0))
        dv(lambda e: e.memset(bzp, 0.0))
        for c in range(4):
            dv(lambda e, c=c: e.tensor_copy(
                bz[:, 1024 * c:1024 * c + 1152].rearrange("p (j w) -> p j w", w=144)[:, :, 0:16], bt[:, 8 * c:8 * c + 8, :]))
            dv(lambda e, c=c: e.tensor_copy(
                bzp[:, 1024 * c:1024 * c + 1152].rearrange("p (j w) -> p j w", w=144)[:, :, 0:16], bt2[:, 8 * c:8 * c + 8, :]))
        for src, dstL, nm in [(bz, LX, "LX"), (bzp, LXp, "LXp")]:
            for g4 in range(8):
                pb = g4 % 2

                def f(e, src=src, g4=g4, pb=pb):
                    ins = None
                    for i in range(4):
                        g = g4 * 4 + i
                        ins = e.transpose(ps[pb][:, i * 128:(i + 1) * 128], src[:, g * 128:(g + 1) * 128], ident_f)
                    return ins
                S.op("pe", f, reads=["ssmp", "consts"], writes=[psk[pb]])
                S.op("act", lambda e, dstL=dstL, g4=g4, pb=pb: e.activation(
                    dstL[:, g4 * 4:g4 * 4 + 4, :], ps[pb][:].rearrange("p (g n) -> p g n", g=4), AF.Copy),
                    reads=[psk[pb]], writes=[nm])
        cs3a = csa.rearrange("p (c n) -> p c n", c=4)
        cs3b = csb.rearrange("p (c n) -> p c n", c=4)
        dv(lambda e: e.tensor_scalar(cs3a[:, :, 64:128], cs3a[:, :, 64:128], -1.0, None, ALU.mult))
        dv(lambda e: e.tensor_scalar(csb, csb, -1.0, None, ALU.mult))
        for src, dstL, nm in [(csa, LY1, "LY1"), (csb, LY2, "LY2")]:
            S.op("pool", lambda e, dstL=dstL: e.memset(dstL, 0.0), reads=[], writes=[nm])

            def f(e, src=src):
                ins = None
                for c in range(4):
                    ins = e.transpose(ps[2][:, c * 128:(c + 1) * 128], src[:, c * 128:(c + 1) * 128], ident_f)
                return ins
            S.op("pe", f, reads=["ssmp", "consts"], writes=[psk[2]])
            S.op("dve", lambda e: e.tensor_copy(lyd, ps[2][:]), reads=[psk[2]], writes=["lyd"])
            for c in range(4):
                S.op("dve", lambda e, c=c, dstL=dstL: e.tensor_copy(
                    dstL.rearrange("p g n -> p (g n)")[:, 1024 * c:1024 * c + 1152].rearrange("p (j w) -> p j w", w=144)[:, :, 0:16],
                    lyd[:, c * 128:(c + 1) * 128].rearrange("p (j w) -> p j w", w=16)),
                    reads=["lyd"], writes=[nm])

        S.fence()
        checkpoint("C0")
        yT = uT
        ADD_ENG = os.environ.get("KADD", "pool")
        P2_ENG = os.environ.get("KP2", "dve")
        for g in range(G):
            c = g // 8
            gi = g % 8
            S.op("act", lambda e, g=g: e.activation(tmall, tcol, AF.Exp, bias=p["a2048"][:, g:g + 1], scale=p["negA"][:, g:g + 1]),
                 reads=["ssmp", "ssmc"], writes=TMK)
            S.op("act", lambda e: e.activation(ctmp[:, 1:2], halfpi, AF.Sin), reads=["negpi"], writes=["actwarm"])
            wvi = wv.bitcast(I32)
            SPL = 1408
            pieces = [(slice(0, SPL), ["bqc0", "bqc1", "bqc2"]), (slice(SPL, QT), ["bqc2", "bqc3"])]
            for hh in range(2):
                cs_, bcs = pieces[hh]
                bk, wk = f"bqh{hh}", f"wvh{hh}"
                S.op("dve", lambda e, g=g, cs_=cs_: e.tensor_scalar(bq[:, cs_], tcol[:, cs_], p["thm"][:, g:g + 1], None, ALU.mult),
                     reads=["ssmp", "ssmc"], writes=[bk, "bq"] + bcs)
                S.op("dve", lambda e, cs_=cs_: e.tensor_scalar(wvi[:, cs_], bq[:, cs_], 1.0 / TWO_PI, None, ALU.mult),
                     reads=[bk], writes=[wk, "wv"])
                S.op("dve", lambda e, cs_=cs_: e.scalar_tensor_tensor(bq[:, cs_], wvi[:, cs_], -TWO_PI, bq[:, cs_], ALU.mult, ALU.add),
                     reads=[wk, bk], writes=[bk])
                S.op("dve", lambda e, cs_=cs_: e.tensor_scalar(bq[:, cs_], bq[:, cs_], PI, -PI, ALU.min, ALU.max),
                     reads=[bk], writes=[bk])
                S.op("act", lambda e, cs_=cs_: e.activation(S1[:, cs_], bq[:, cs_], AF.Sin), reads=[bk], writes=[f"S1h{hh}", "S1"])
                S.op("act", lambda e, cs_=cs_: e.activation(wv[:, cs_], bq[:, cs_], AF.Abs), reads=[bk, wk], writes=[wk])
                S.op("act", lambda e, cs_=cs_: e.activation(C1[:, cs_], wv[:, cs_], AF.Sin, bias=halfpi, scale=-1.0),
                     reads=[wk, "negpi"], writes=[f"C1h{hh}", "C1"])
            S.op("dve", lambda e: e.memset(carry, 0.0), reads=[], writes=["carry"])
            for hh in range(2):
                cs_, bcs = pieces[hh]
                bk, wk = f"bqh{hh}", f"wvh{hh}"
                S.op("dve", lambda e, cs_=cs_: e.tensor_tensor(wv[:, cs_], tmall[:, cs_], S1[:, cs_], ALU.mult),
                     reads=TMK + [f"S1h{hh}"], writes=[wk, "wv"])
                S.op("dve", lambda e, cs_=cs_: e.tensor_tensor(bq[:, cs_], tmall[:, cs_], C1[:, cs_], ALU.mult),
                     reads=TMK + [f"C1h{hh}"], writes=[bk, "bq"] + bcs)
            zts = [zt, zt2]

            def acc_ops(k, ntc, g=g, c=c):
                for tc in range(ntc):
                    sl = tc % 2
                    cols = slice(tc * 512, (tc + 1) * 512)
                    S.op("pe", lambda e, cols=cols, sl=sl: e.matmul(
                        ps[sl][:], LX[:, g, :], uT[:, c, k, cols], start=True, stop=True),
                        reads=["LX", f"uT{c}_{k}"], writes=[psk[sl]])
                    S.op("pe", lambda e, cols=cols: e.matmul(
                        ps[2][:], LXp[:, g, :], uT[:, c, k, cols], start=True, stop=True),
                        reads=["LXp", f"uT{c}_{k}"], writes=[psk[2]])
                    S.op("dve", lambda e, cols=cols, sl=sl, tc=tc: e.scalar_tensor_tensor(
                        tm1[sl], ps[sl][:], 1.0, bq[:, cols], ALU.mult, ALU.mult, accum_out=zacc[:, 2 * tc:2 * tc + 1]),
                        reads=[psk[sl], "bq"], writes=[f"tm1{sl}", "zacc"])
                    S.op("dve", lambda e, cols=cols, sl=sl, tc=tc: e.scalar_tensor_tensor(
                        tm2[sl], ps[2][:], 1.0, wv[:, cols], ALU.mult, ALU.mult, accum_out=zacc[:, 2 * tc + 1:2 * tc + 2]),
                        reads=[psk[2], "wv"], writes=[f"tm2{sl}", "zacc"])

            def reduce_to(zi, ncols):
                S.op("dve", lambda e: e.reduce_sum(zts[zi][:, 1:2], zacc[:, 0:ncols], AX.X), reads=["zacc"], writes=[f"zt{zi}"])

            def reframe_from(zi, k, g=g):
                S.op("pe", lambda e: e.matmul(ps[2][:, 510:512], swapj, zts[zi][:, 0:2], start=True, stop=True),
                     reads=[f"zt{zi}", "consts"], writes=[psk[2]])
                S.op("dve", lambda e: e.tensor_scalar(ctmp[:, 0:1], zts[zi][:, 1:2], fAk[k][:, g:g + 1], None, ALU.mult),
                     reads=[f"zt{zi}", "ssmp"], writes=["ctmp"])
                S.op("dve", lambda e: e.scalar_tensor_tensor(
                    carry[:, 0:1], ps[2][:, 511:512], fBk[k][:, g:g + 1], ctmp[:, 0:1], ALU.mult, ALU.add),
                    reads=[psk[2], "ctmp", "ssmp"], writes=["carry"])

            acc_ops(0, 4)
            reduce_to(0, 8)
            acc_ops(1, 4)
            reframe_from(0, 0)
            reduce_to(1, 8)
            S.op("dve", lambda e, g=g: e.scalar_tensor_tensor(
                zt2[:, 1:2], carry[:, 0:1], p["R"][:, g:g + 1], zt2[:, 1:2], ALU.mult, ALU.add),
                reads=["carry", "zt1", "ssmp"], writes=["zt1"])
            for k in range(2, 4):
                def mults(tc, g=g, c=c, k=k):
                    sl = tc % 2
                    cols = slice(tc * 512, (tc + 1) * 512)
                    S.op("pe", lambda e: e.matmul(
                        ps[sl][:], LX[:, g, :], uT[:, c, k, cols], start=True, stop=True),
                        reads=["LX", f"uT{c}_{k}"], writes=[psk[sl]])
                    S.op("pe", lambda e: e.matmul(
                        ps[2][:], LXp[:, g, :], uT[:, c, k, cols], start=True, stop=True),
                        reads=["LXp", f"uT{c}_{k}"], writes=[psk[2]])
                    S.op("dve", lambda e: e.tensor_tensor(tm1[sl], ps[sl][:], C1[:, cols], ALU.mult),
                         reads=[psk[sl], "C1"], writes=[f"tm1{sl}"])
                    S.op("dve", lambda e: e.tensor_tensor(tm2[sl], ps[2][:], S1[:, cols], ALU.mult),
                         reads=[psk[2], "S1"], writes=[f"tm2{sl}"])
                    S.op(ADD_ENG if tc < 3 else "dve", lambda e: e.tensor_tensor(bq[:, cols], tm1[sl], tm2[sl], ALU.add),
                         reads=[f"tm1{sl}", f"tm2{sl}"], writes=[f"bqc{tc}", "bq"])

                def cscan(tc, g=g, from_carry=False):
                    cols = slice(tc * 512, (tc + 1) * 512)
                    init = carry[:, 0:1] if (tc == 0 or from_carry) else wv[:, tc * 512 - 1:tc * 512]
                    S.op("dve", lambda e: e.tensor_tensor_scan(
                        wv[:, cols], p["rr"][:, g:g + 1].to_broadcast([128, 512]), bq[:, cols], init, ALU.mult, ALU.add),
                        reads=[f"bqc{tc}", "carry", "wv", "ssmp"], writes=["wv"])
                if k == 2:
                    acc_ops(2, 3)
                    reframe_from(1, 1)
                    reduce_to(0, 6)
                    S.op("dve", lambda e, g=g: e.tensor_scalar(ctmp[:, 0:1], carry[:, 0:1], p["R1536"][:, g:g + 1], None, ALU.mult),
                         reads=["carry", "ssmp"], writes=["ctmp"])
                    S.op("dve", lambda e, g=g: e.scalar_tensor_tensor(
                        carry[:, 0:1], zt[:, 1:2], p["Rm512"][:, g:g + 1], ctmp[:, 0:1], ALU.mult, ALU.add),
                        reads=["zt0", "ctmp", "ssmp"], writes=["carry"])
                    mults(3)
                    cscan(3, from_carry=True)
                else:
                    mults(0)
                    mults(1)
                    mults(2)
                    cscan(0)
                    mults(3)
                    cscan(1)
                    cscan(2)
                    cscan(3)
                if k < 3:
                    S.op("pe", lambda e: e.matmul(ps[2][:, 510:512], swapj, wv[:, QT - 2:QT], start=True, stop=True),
                         reads=["wv", "consts"], writes=[psk[2]])
                    S.op("dve", lambda e, g=g, k=k: e.tensor_scalar(ctmp[:, 0:1], wv[:, QT - 1:QT], fAk[k][:, g:g + 1], None, ALU.mult),
                         reads=["wv", "ssmp"], writes=["ctmp"])
                    S.op("dve", lambda e, g=g, k=k: e.scalar_tensor_tensor(
                        carry[:, 0:1], ps[2][:, 511:512], fBk[k][:, g:g + 1], ctmp[:, 0:1], ALU.mult, ALU.add),
                        reads=[psk[2], "ctmp", "ssmp"], writes=["carry"])
                if k >= 2:
                    lo = QT - 128 if k == 2 else 0
                    S.op("dve", lambda e, lo=lo: e.tensor_tensor(P1[:, lo:QT], wv[:, lo:QT], C1[:, lo:QT], ALU.mult),
                         reads=["wv", "C1"], writes=["P1"])
                    S.op(P2_ENG, lambda e, lo=lo: e.tensor_tensor(P2[:, lo:QT], wv[:, lo:QT], S1[:, lo:QT], ALU.mult),
                         reads=["wv", "S1"], writes=["P2"])
                    if k == 2:
                        def fh(e, g=g, gi=gi):
                            e.matmul(ps[3][:, 0:128], LY1[:, g, :], P1[:, QT - 128:QT], start=(gi == 0), stop=False)
                            return e.matmul(ps[3][:, 0:128], LY2[:, g, :], P2[:, QT - 128:QT], start=False, stop=(gi == 7))
                        S.op("pe", fh, reads=["P1", "P2", "LY1", "LY2"], writes=[psk[3]])
                    else:
                        def fo(e, g=g, gi=gi):
                            ins = None
                            for tc in range(4):
                                cols = slice(tc * 512, (tc + 1) * 512)
                                e.matmul(ps[4 + tc][:], LY1[:, g, :], P1[:, cols], start=(gi == 0), stop=False)
                                ins = e.matmul(ps[4 + tc][:], LY2[:, g, :], P2[:, cols], start=False, stop=(gi == 7))
                            return ins
                        S.op("pe", fo, reads=["P1", "P2", "LY1", "LY2"], writes=[psk[4], psk[5], psk[6], psk[7]])
            if gi == 7:
                yc = uT[:, c, 0:2, :].rearrange("p k t -> p (k t)").bitcast(F32)
                yh = uT[:, c, 2, 0:256].bitcast(F32)
                S.op("dve", lambda e, c=c, yh=yh: e.scalar_tensor_tensor(
                    yh, uT[:, c, 2, QT - 128:QT], dsk[:, c:c + 1], ps[3][:, 0:128], ALU.mult, ALU.add),
                    reads=[psk[3], f"uT{c}_2", "consts"], writes=[f"yhalo{c}"])
                for tc in range(4):
                    S.op("dve", lambda e, c=c, tc=tc, yc=yc: e.scalar_tensor_tensor(
                        yc[:, tc * 512:(tc + 1) * 512], uT[:, c, 3, tc * 512:(tc + 1) * 512], dsk[:, c:c + 1],
                        ps[4 + tc][:], ALU.mult, ALU.add),
                        reads=[psk[4 + tc], f"uT{c}_3", f"uT{c}_0", f"uT{c}_1", "consts"], writes=[f"yT{c}"])

        checkpoint("C")
        S.fence()
        AR.top = R_A
        woutb = AR.bf(8 * D).rearrange("p (c n) -> p c n", c=8)
        wglub = AR.bf(4 * 512).rearrange("p (c n) -> p c n", c=4)
        wstD = AR.f32(8 * 256).rearrange("p (c n) -> p c n", c=8)
        assert AR.top <= R_U
        AR.top = R_Q
        hn2T = AR.bf(8 * NTOK).rearrange("p (c t) -> p c t", c=8)
        R_H = AR.top
        CH = 512
        gx2 = AR.f32(4 * CH).rearrange("p (c t) -> p c t", c=4)
        gyg = AR.f32(4 * CH).rearrange("p (c t) -> p c t", c=4)
        gyb = AR.bf(4 * CH).rearrange("p (c t) -> p c t", c=4)
        gsg = AR.f32(4 * CH).rearrange("p (c t) -> p c t", c=4)
        grs = AR.f32(CH)
        assert AR.top <= NW
        AR.top = R_H
        xt2 = [AR.f32(1024) for _ in range(2)]
        ht2 = [AR.f32(1024) for _ in range(2)]
        hb2 = [AR.bf(1024) for _ in range(2)]
        st2 = [AR.f32(2) for _ in range(2)]
        assert AR.top <= NW
        wglu_v = wglu_d.rearrange("(c p) n -> p c n", p=128)
        for hf in range(2):
            S.dma("sp", wstD[:, 0:4, :], wglu_v[:, :, hf * 256:(hf + 1) * 256], writes=["wstD"])
            S.op("pool", lambda e, hf=hf: e.tensor_copy(wglub[:, :, hf * 256:(hf + 1) * 256], wstD[:, 0:4, :]),
                 reads=["wstD"], writes=["wglub"])
        wout_v = wout_d.rearrange("(c p) n -> p c n", p=128)
        for hf in range(4):
            S.dma("sp", wstD, wout_v[:, :, hf * 256:(hf + 1) * 256], writes=["wstD"])
            S.op("pool", lambda e, hf=hf: e.tensor_tensor(
                woutb[:, :, hf * 256:(hf + 1) * 256], wstD, gmix.unsqueeze(2).to_broadcast([128, 8, 256]), ALU.mult),
                reads=["wstD", "consts"], writes=["woutb"])
        chunks = [(0, 128)] + [(128 + i * CH, CH) for i in range(QT // CH)]
        for ci, (t0, n) in enumerate(chunks):
            def ysrc(c, ci=ci, n=n):
                if ci == 0:
                    return uT[:, c, 2, 0:256].bitcast(F32)
                yc = uT[:, c, 0:2, :].rearrange("p k t -> p (k t)").bitcast(F32)
                return yc[:, (ci - 1) * CH:ci * CH]
            ykeys = [f"yhalo{c}" if ci == 0 else f"yT{c}" for c in range(4)]
            for c in range(4):
                S.op("dve", lambda e, c=c, n=n, ysrc=ysrc: e.tensor_tensor(gx2[:, c, 0:n], ysrc(c), ysrc(c), ALU.mult),
                     reads=[ykeys[c]], writes=[f"gx2{c}"])
                S.op("dve", lambda e, c=c, n=n: e.tensor_scalar(gx2[:, c, 0:n], gx2[:, c, 0:n], 0.044715, 1.0, ALU.mult, ALU.add),
                     reads=[f"gx2{c}"], writes=[f"gx2{c}"])
                S.op("dve", lambda e, c=c, n=n, ysrc=ysrc: e.tensor_tensor(gx2[:, c, 0:n], gx2[:, c, 0:n], ysrc(c), ALU.mult),
                     reads=[f"gx2{c}", ykeys[c]], writes=[f"gx2{c}"])
                S.op("act", lambda e, c=c, n=n: e.activation(gx2[:, c, 0:n], gx2[:, c, 0:n], AF.Sigmoid, scale=1.5957691216057308),
                     reads=[f"gx2{c}"], writes=[f"gx2{c}"])
                S.op("dve", lambda e, c=c, n=n, ysrc=ysrc: e.tensor_tensor(gyg[:, c, 0:n], gx2[:, c, 0:n], ysrc(c), ALU.mult),
                     reads=[f"gx2{c}", ykeys[c]], writes=[f"gyg{c}"])
                S.op("act", lambda e, c=c, n=n: e.activation(gyb[:, c, 0:n], gyg[:, c, 0:n], AF.Copy), reads=[f"gyg{c}"], writes=[f"gyb{c}"])
            for nc_ in range(4):
                pb = nc_ % 2
                S.op("pe", lambda e, nc_=nc_, pb=pb, n=n: [e.matmul(
                    ps[pb][:, 0:n], wglub[:, kc, nc_ * 128:(nc_ + 1) * 128], gyb[:, kc, 0:n],
                    start=(kc == 0), stop=(kc == 3)) for kc in range(4)][-1],
                    reads=["wglub"] + [f"gyb{c}" for c in range(4)], writes=[psk[pb]])
                S.op("act", lambda e, nc_=nc_, pb=pb, n=n: e.activation(
                    gsg[:, nc_, 0:n], ps[pb][:, 0:n], AF.Sigmoid, bias=bglu[:, nc_:nc_ + 1]),
                    reads=[psk[pb], "consts"], writes=[f"gsg{nc_}"])
                S.op("dve", lambda e, nc_=nc_, n=n: e.tensor_tensor(gsg[:, nc_, 0:n], gsg[:, nc_, 0:n], gyg[:, nc_, 0:n], ALU.mult),
                     reads=[f"gsg{nc_}", f"gyg{nc_}"], writes=[f"gsg{nc_}"])
                S.op("dve", lambda e, nc_=nc_, n=n: e.tensor_tensor(gx2[:, nc_, 0:n], gsg[:, nc_, 0:n], gsg[:, nc_, 0:n], ALU.mult),
                     reads=[f"gsg{nc_}"], writes=[f"gx2{nc_}"])
            S.op("pe", lambda e, n=n: [e.matmul(ps[2][:, 0:n], onesm, gx2[:, kc, 0:n], start=(kc == 0), stop=(kc == 3))
                                       for kc in range(4)][-1],
                 reads=["consts"] + [f"gx2{c}" for c in range(4)], writes=[psk[2]])
            S.op("dve", lambda e, n=n: e.tensor_scalar(grs[:, 0:n], ps[2][:, 0:n], 1.0 / 512, EPS, ALU.mult, ALU.add),
                 reads=[psk[2]], writes=["grs"])
            rsqrt_ops(grs[:, 0:n], "grs")
            for nc_ in range(4):
                S.op("dve", lambda e, nc_=nc_, n=n, t0=t0: e.tensor_tensor(
                    mixT[:, 4 + nc_, t0:t0 + n], gsg[:, nc_, 0:n], grs[:, 0:n], ALU.mult),
                    reads=[f"gsg{nc_}", "grs"], writes=[f"mixS{ci}"])
        S.fence()

        def op_stageA(n):
            sl = n % 2
            src = xw[2, QT - 128:QT, :] if n == 0 else xw[3, (n - 1) * 128:n * 128, :]
            S.dma("sp", xt2[sl], src, writes=[f"xt2{sl}"])
            ci = 0 if n == 0 else 1 + (n - 1) // 4
            for hf in range(2):
                S.op("pe", lambda e, n=n, hf=hf: [e.matmul(
                    ps[4 + hf][:], mixT[:, f, n * 128:(n + 1) * 128], woutb[:, f, hf * 512:(hf + 1) * 512],
                    start=(f == 0), stop=(f == 7)) for f in range(8)][-1],
                    reads=[f"mixA{n}", f"mixS{ci}", "woutb"], writes=[psk[4 + hf]])
                S.op("dve", lambda e, sl=sl, hf=hf: e.tensor_tensor(
                    ht2[sl][:, hf * 512:(hf + 1) * 512], ps[4 + hf][:], xt2[sl][:, hf * 512:(hf + 1) * 512], ALU.add),
                    reads=[psk[4 + hf], f"xt2{sl}"], writes=[f"ht2{sl}"])
            S.dma("sp", hs_d[n * 128:(n + 1) * 128, :], ht2[sl], reads=[f"ht2{sl}"], writes=[f"hs{n}"], key=f"hsw{sl}")
            S.op("dve", lambda e, sl=sl: e.scalar_tensor_tensor(hb2[sl], ht2[sl], 1.0, ht2[sl], ALU.mult, ALU.mult,
                                                               accum_out=st2[sl][:, 0:1]),
                 reads=[f"ht2{sl}"], writes=[f"hb2{sl}", f"st2{sl}"])
            S.op("dve", lambda e, sl=sl: e.tensor_scalar(st2[sl][:, 1:2], st2[sl][:, 0:1], 1.0 / D, EPS, ALU.mult, ALU.add),
                 reads=[f"st2{sl}"], writes=[f"rs2{sl}"])
            rsqrt_ops(st2[sl][:, 1:2], f"rs2{sl}")
            S.op("dve", lambda e, sl=sl: e.tensor_scalar(hb2[sl], ht2[sl], st2[sl][:, 1:2], None, ALU.mult),
                 reads=[f"ht2{sl}", f"rs2{sl}"], writes=[f"hb2{sl}"])

        def op_stageB(n):
            sl = n % 2
            transpose8(hb2[sl], 6 + (n % 2), lambda n=n: hn2T[:, :, n * 128:(n + 1) * 128], [f"hb2{sl}"], [f"hn2T{n}"])

        op_stageA(0)
        for n in range(17):
            if n + 1 < 17:
                op_stageA(n + 1)
            op_stageB(n)
        checkpoint("D")
        S.fence()
        AR.top = R_A
        actT = AR.bf(NFC * QT).rearrange("p (f t) -> p f t", f=NFC)
        assert AR.top <= R_Q
        AR.top = R_H
        wus = [AR.f32(8 * 256).rearrange("p (c n) -> p c n", c=8) for _ in range(1)]
        wub = [AR.bf(8 * 256).rearrange("p (c n) -> p c n", c=8) for _ in range(2)]
        Gb = [AR.f32(QT + 2), None]
        Vb = [AR.f32(QT + 2), None]
        assert AR.top <= NW
        AR.top = R_MIX
        Gb[1] = AR.f32(QT + 2)
        Vb[1] = AR.f32(QT + 2)
        accg = AR.f32(QT)
        accv = [AR.f32(QT), None]
        assert AR.top <= persist_top
        AR.top = R_A + NFC * QT // 2
        assert R_Q - AR.top >= 0
        gap = R_Q - AR.top
        accv[1] = None
        flag2 = flags[:, 2:3]
        cw3 = cw.rearrange("p (c k) -> p c k", k=3)
        hkeys_all = [f"hn2T{n}" for n in range(17)]
        def emit_w(fc):
            sl = fc % 2
            S.dma("sp", wus[0].rearrange("p c n -> p (c n)"), wup_d[fc], writes=["wus0"])
            for dc in range(8):
                S.op("act", lambda e, sl=sl, dc=dc: e.activation(wub[sl][:, dc, :], wus[0][:, dc, :], AF.Copy, scale=g2[:, dc:dc + 1]),
                     reads=["wus0", "consts"], writes=[f"wub{sl}"])

        def emit_mm(fc):
            sl = fc % 2
            if fc == 0:
                emit_w(0)
            pend = []
            if fc + 1 < NFC:
                S.dma("sp", wus[0].rearrange("p c n -> p (c n)"), wup_d[fc + 1], writes=["wus0"])
                pend = list(range(8))
            for gv in range(2):
                dstb = Gb[sl] if gv == 0 else Vb[sl]
                bkey = f"Gb{sl}" if gv == 0 else f"Vb{sl}"
                wcols = slice(gv * 128, (gv + 1) * 128)
                pbh = 4 + gv
                S.op("pe", lambda e, sl=sl, wcols=wcols, pbh=pbh: [e.matmul(
                    ps[pbh][:, 0:2], wub[sl][:, dc, wcols], hn2T[:, dc, 126:128], start=(dc == 0), stop=(dc == 7))
                    for dc in range(8)][-1],
                    reads=[f"wub{sl}", "hn2T0"], writes=[psk[pbh]])
                S.op("act", lambda e, dstb=dstb, pbh=pbh: e.activation(dstb[:, 0:2], ps[pbh][:, 0:2], AF.Copy, scale=flag2),
                     reads=[psk[pbh], "consts", bkey], writes=[bkey + "h"])
                for tc in range(4):
                    pb = gv * 2 + tc % 2
                    S.op("pe", lambda e, sl=sl, wcols=wcols, pb=pb, tc=tc: [e.matmul(
                        ps[pb][:], wub[sl][:, dc, wcols], hn2T[:, dc, 128 + tc * 512:128 + (tc + 1) * 512],
                        start=(dc == 0), stop=(dc == 7)) for dc in range(8)][-1],
                        reads=[f"wub{sl}"] + hkeys_all[1 + tc * 4:5 + tc * 4], writes=[psk[pb]])
                    S.op("act", lambda e, dstb=dstb, pb=pb, tc=tc: e.activation(
                        dstb[:, 2 + tc * 512:2 + (tc + 1) * 512], ps[pb][:], AF.Copy),
                        reads=[psk[pb]], writes=[bkey])
                    if pend:
                        dc = pend.pop(0)
                        nsl = (fc + 1) % 2
                        S.op("act", lambda e, nsl=nsl, dc=dc: e.activation(wub[nsl][:, dc, :], wus[0][:, dc, :], AF.Copy, scale=g2[:, dc:dc + 1]),
                             reads=["wus0", "consts"], writes=[f"wub{nsl}"])
        def emit_conv(fc):
            sl = fc % 2
            for gv in range(2):
                dstb = Gb[sl] if gv == 0 else Vb[sl]
                bkey = f"Gb{sl}" if gv == 0 else f"Vb{sl}"
                acc = accg if gv == 0 else accv[0]
                akey = "accg" if gv == 0 else "accv"
                ch = fc + gv * NFC
                S.op("dve", lambda e, acc=acc, dstb=dstb, ch=ch: e.tensor_scalar(
                    acc, dstb[:, 2:QT + 2], cw3[:, ch, 2:3], cb[:, ch:ch + 1], ALU.mult, ALU.add),
                    reads=[bkey, "consts"], writes=[akey])
                S.op("dve", lambda e, acc=acc, dstb=dstb, ch=ch: e.scalar_tensor_tensor(
                    acc, dstb[:, 1:QT + 1], cw3[:, ch, 1:2], acc, ALU.mult, ALU.add),
                    reads=[bkey, bkey + "h", akey, "consts"], writes=[akey])
                if gv == 0:
                    S.op("dve", lambda e, acc=acc, dstb=dstb, ch=ch: e.scalar_tensor_tensor(
                        acc, dstb[:, 0:QT], cw3[:, ch, 0:1], acc, ALU.mult, ALU.add),
                        reads=[bkey, bkey + "h", akey, "consts"], writes=[akey])
                else:
                    S.op("dve", lambda e, acc=acc, dstb=dstb, ch=ch: e.scalar_tensor_tensor(
                        acc, dstb[:, 0:QT], cw3[:, ch, 0:1], acc, ALU.mult, ALU.add),
                        reads=[bkey, bkey + "h", akey, "consts"], writes=[akey])
        def emit_fin(fc):
            sl = fc % 2
            S.op("act", lambda e, sl=sl: e.activation(Gb[sl][:, 2:QT + 2], accg, AF.Silu), reads=["accg", f"Gb{sl}"], writes=[f"Gb{sl}"])
            S.op("dve", lambda e, fc=fc, sl=sl: e.tensor_tensor(actT[:, fc, :], Gb[sl][:, 2:QT + 2], accv[0], ALU.mult),
                 reads=[f"Gb{sl}", "accv"], writes=[f"actT{fc}"])

        for fc in range(NFC + 1):
            if fc < NFC:
                emit_mm(fc)
            if fc >= 1:
                emit_fin(fc - 1)
            if fc < NFC:
                emit_conv(fc)

        checkpoint("E1")
        S.fence()
        AR.top = R_Q
        wdb = AR.bf(NFC * D).rearrange("p (f n) -> p f n", f=NFC)
        wds = [AR.f32(2 * D).rearrange("p (f n) -> p f n", f=2) for _ in range(2)]
        lnf = AR.f32(D)
        AR.top = R_MIX
        hr = [AR.f32(D) for _ in range(2)]
        ob = [AR.f32(D) for _ in range(2)]
        osq = AR.bf(D)
        st3 = [AR.f32(2) for _ in range(2)]
        assert AR.top <= persist_top
        S.dma("sp", lnf, lnf_d, writes=["lnf"])
        wd_v = wdown_d.rearrange("(f p) n -> p f n", p=128)
        for f2 in range(NFC // 2):
            sl = f2 % 2
            S.dma("sp", wds[sl], wd_v[:, 2 * f2:2 * f2 + 2, :], writes=[f"wds{sl}"])
            if f2 % 2 == 0:
                S.op("pool", lambda e, sl=sl, f2=f2: e.tensor_copy(wdb[:, 2 * f2:2 * f2 + 2, :], wds[sl]),
                     reads=[f"wds{sl}"], writes=[f"wdb{f2}"])
            else:
                S.op("act", lambda e, sl=sl, f2=f2: e.activation(wdb[:, 2 * f2:2 * f2 + 2, :], wds[sl], AF.Copy),
                     reads=[f"wds{sl}"], writes=[f"wdb{f2}"])
        for n in range(16):
            sl = n % 2
            S.dma("sp", hr[sl], hs_d[(n + 1) * 128:(n + 2) * 128, :], reads=[f"hs{n + 1}"], writes=[f"hr{sl}"])
            wkeys = [f"wdb{f2}" for f2 in range(NFC // 2)]
            if n == 0:
                for f2 in range(NFC // 2):
                    for hf in range(2):
                        pb = hf
                        S.op("pe", lambda e, f2=f2, hf=hf, pb=pb: [e.matmul(
                            ps[pb][:], actT[:, f, 0:128], wdb[:, f, hf * 512:(hf + 1) * 512],
                            start=(f == 0), stop=(f == NFC - 1)) for f in (2 * f2, 2 * f2 + 1)][-1],
                            reads=[f"wdb{f2}", f"actT{2 * f2}", f"actT{2 * f2 + 1}"], writes=[psk[pb]])
            for hf in range(2):
                pb = (n % 2) * 2 + hf
                if n > 0:
                    S.op("pe", lambda e, n=n, hf=hf, pb=pb: [e.matmul(
                        ps[pb][:], actT[:, f, n * 128:(n + 1) * 128], wdb[:, f, hf * 512:(hf + 1) * 512],
                        start=(f == 0), stop=(f == NFC - 1)) for f in range(NFC)][-1],
                        reads=wkeys + [f"actT{f}" for f in range(NFC)], writes=[psk[pb]])
                S.op("dve", lambda e, sl=sl, hf=hf, pb=pb: e.tensor_tensor(
                    ob[sl][:, hf * 512:(hf + 1) * 512], ps[pb][:], hr[sl][:, hf * 512:(hf + 1) * 512], ALU.add),
                    reads=[psk[pb], f"hr{sl}"], writes=[f"ob{sl}"])
            S.op("dve", lambda e, sl=sl: e.scalar_tensor_tensor(osq, ob[sl], 1.0, ob[sl], ALU.mult, ALU.mult,
                                                               accum_out=st3[sl][:, 0:1]),
                 reads=[f"ob{sl}"], writes=["osq", f"st3{sl}"])
            S.op("dve", lambda e, sl=sl: e.tensor_scalar(st3[sl][:, 1:2], st3[sl][:, 0:1], 1.0 / D, EPS, ALU.mult, ALU.add),
                 reads=[f"st3{sl}"], writes=[f"rs3{sl}"])
            rsqrt_ops(st3[sl][:, 1:2], f"rs3{sl}")
            S.op("dve", lambda e, sl=sl: e.scalar_tensor_tensor(ob[sl], ob[sl], st3[sl][:, 1:2], lnf, ALU.mult, ALU.mult),
                 reads=[f"ob{sl}", f"rs3{sl}", "lnf"], writes=[f"ob{sl}"])
            S.dma("sp", out_d[n * 128:(n + 1) * 128, :], ob[sl], reads=[f"ob{sl}"], writes=[f"out{n}"], key=f"outw{sl}")
        S.fence()
    except _Stop:
        pass

    dkeys = list(S.dcnt.keys())
    dsem = {k: es.enter_context(nc.semaphore(f"dsem{i}")) for i, k in enumerate(dkeys)}

    def handle(h):
        return esem[h] if isinstance(h, str) else dsem[h]

    def run(name, eng):
        for waits, fn, inc in S.q[name]:
            for h, v in waits:
                eng.wait_ge(handle(h), v)
            if fn is None:
                continue
            ins = fn(eng)
            if isinstance(ins, (list, tuple)):
                ins = ins[-1]
            ins.then_inc(handle(inc), 1 if isinstance(inc, str) else 16)

    with nc.Block() as block:
        @block.tensor
        def _(e):
            run("pe", e)

        @block.scalar
        def _(e):
            run("act", e)

        @block.vector
        def _(e):
            run("dve", e)

        @block.gpsimd
        def _(e):
            run("pool", e)

        @block.sync
        def _(e):
            run("sp", e)
    es.close()
    return nc


def _host_inputs(inp):
    f32 = np.float32
    x = np.asarray(inp["x"], f32)
    w_in = np.asarray(inp["w_in"], f32)[0]
    b_in = np.asarray(inp["b_in"], f32)[0]

    def pcol(v):
        return np.ascontiguousarray(np.asarray(v, f32).reshape(-1, 128).T)

    cols = []
    zero_col = -1
    for j in range(4):
        cols += list(range(j * 64, j * 64 + 64)) + list(range((4 + j) * 64, (4 + j) * 64 + 64))
    cols += list(range(512, 640)) + list(range(640, 768)) + list(range(768, 1280))

    def partner(base):
        c = [zero_col] * 64
        for i in range(8):
            c[i] = base + 8 + i
            c[8 + i] = base + i
        return c
    for j in range(4):
        cols += partner(j * 64) + partner((4 + j) * 64)
    cols += partner(512) + partner(576)
    cols = np.array(cols)
    w_ext = np.concatenate([w_in, np.zeros((D, 1), f32)], axis=1)
    b_ext = np.concatenate([b_in, np.zeros((1,), f32)])
    win = np.ascontiguousarray(w_ext[:, cols].reshape(8, 128, 15, 128).transpose(2, 1, 0, 3).reshape(15, 128, 1024))
    ball = pcol(b_ext[cols])
    shared = dict(
        win=win, ball=ball, g1=pcol(inp["ln1_g"][0]),
        bvb=np.ascontiguousarray(np.broadcast_to(b_in[640:768][None, :], (128, 128))).astype(f32),
        identb=np.eye(128, dtype=f32).astype(ml_dtypes.bfloat16), identf=np.eye(128, dtype=f32),
        swapj=np.roll(np.eye(128, dtype=f32), 64, axis=1), onesm=np.ones((128, 128), f32),
    )
    sinks = np.asarray(inp["sinks"], f32)[0]
    shared["sinkb"] = np.ascontiguousarray(np.broadcast_to(sinks[None, :], (128, 8))).astype(f32)
    lre = np.asarray(inp["lam_re"], f32)[0].T
    lim = np.asarray(inp["lam_im"], f32)[0].T
    shared["lre"] = np.ascontiguousarray(np.concatenate([lre, lre], 0))
    shared["lim"] = np.ascontiguousarray(np.concatenate([lim, lim], 0))
    shared["lstep"] = np.ascontiguousarray(np.broadcast_to(np.asarray(inp["log_step"], f32)[0][None, :], (128, G))).astype(f32)
    shared["sgn"] = np.concatenate([-np.ones((64, 1), f32), np.ones((64, 1), f32)], 0)
    bre = np.asarray(inp["ssm_b_re"], f32)[0].transpose(1, 0, 2).reshape(64, G * 16)
    bim = np.asarray(inp["ssm_b_im"], f32)[0].transpose(1, 0, 2).reshape(64, G * 16)
    shared["bs1"] = np.ascontiguousarray(np.concatenate([bre, bim], 0))
    shared["bs2"] = np.ascontiguousarray(np.concatenate([bim, bre], 0))
    cre = np.asarray(inp["ssm_c_re"], f32)[0].reshape(4, 128, 64)
    cim = np.asarray(inp["ssm_c_im"], f32)[0].reshape(4, 128, 64)
    shared["csa"] = np.ascontiguousarray(np.concatenate([cre, cim], 2).transpose(1, 0, 2).reshape(128, 512))
    shared["csb"] = np.ascontiguousarray(np.concatenate([cim, cre], 2).transpose(1, 0, 2).reshape(128, 512))
    shared["dsk"] = pcol(np.asarray(inp["ssm_d"], f32)[0].reshape(-1))
    shared["tcol"] = np.ascontiguousarray(np.broadcast_to(np.arange(1, QT + 1, dtype=f32)[None, :], (128, QT))).astype(f32)
    shared["wglu"] = np.ascontiguousarray(np.asarray(inp["w_glu"], f32)[0])
    shared["bglu"] = pcol(inp["b_glu"][0])
    shared["wout"] = np.ascontiguousarray(np.asarray(inp["w_out"], f32)[0])
    shared["gmix"] = pcol(np.concatenate([np.asarray(inp["g_attn"], f32)[0], np.asarray(inp["g_ssm"], f32)[0]]))
    shared["g2"] = pcol(inp["ln2_g"][0])
    wu = np.asarray(inp["w_up"], f32)[0].reshape(8, 128, 2, NFC, 128)
    shared["wup"] = np.ascontiguousarray(wu.transpose(3, 1, 0, 2, 4).reshape(NFC, 128, 2048))
    cwv = np.asarray(inp["conv_w"], f32)[0]
    shared["cw"] = np.ascontiguousarray(cwv.T.reshape(44, 128, 3).transpose(1, 0, 2).reshape(128, 132))
    shared["cb"] = pcol(inp["conv_b"][0])
    shared["wdown"] = np.ascontiguousarray(np.asarray(inp["w_down"], f32)[0])
    shared["lnf"] = np.ascontiguousarray(np.broadcast_to(np.asarray(inp["lnf_g"], f32)[None, :], (128, D))).astype(f32)

    qi = np.arange(128)[:, None]
    kj = np.arange(256)[None, :]
    diff = qi + 128 - kj
    band = (diff >= 0) & (diff < 128)
    m_gen = np.where(band, 0.0, -1e30).astype(f32)
    m_first = np.where(band & (kj >= 128), 0.0, -1e30).astype(f32)
    inv_freq = (500000.0 ** (-np.arange(8, dtype=np.float64) * 2.0 / 16.0))
    maps = []
    for c in range(8):
        b, q = divmod(c, 4)
        xwv = np.zeros((4, QT, D), f32)
        fl = np.zeros((128, 4), f32)
        for k in range(4):
            qq = q - 3 + k
            if qq >= 0:
                xwv[k] = x[b, qq * QT:(qq + 1) * QT]
                fl[:, k] = 1.0
        pos = (q * QT - 512 + np.arange(NPT)).astype(np.float64)
        ang = (pos[None, :].astype(np.float32) * inv_freq[:, None].astype(np.float32)).astype(np.float64)
        cs, sn = np.cos(ang).astype(f32), np.sin(ang).astype(f32)
        rc = np.ones((128, NPT), f32)
        rs = np.zeros((128, NPT), f32)
        for b0 in (0, 64):
            rc[b0:b0 + 8] = cs
            rc[b0 + 8:b0 + 16] = cs
            rs[b0:b0 + 8] = -sn
            rs[b0 + 8:b0 + 16] = sn
        m = dict(shared)
        m.update(xw=xwv, flags=fl, ropec=rc, ropes=rs,
                 mask=np.ascontiguousarray(np.concatenate([m_first if q == 0 else m_gen, m_gen], 1)))
        maps.append(m)
    return maps


_NC_CACHE = {}
DBG_INFO = {}


def kernel(**inputs):
    maps = _host_inputs(inputs)
    if "nc" not in _NC_CACHE:
        st = os.environ.get("KSTOP")
        dm = os.environ.get("KDUMP", "0,16384").split(",")
        _NC_CACHE["nc"] = build_nc(stop=st, dump=(int(dm[0]), int(dm[1])))
    nc = _NC_CACHE["nc"]
    res = run_bass_kernel_spmd(nc, maps, core_ids=list(range(8)))
    out = np.zeros((2, SEQ, D), np.float32)
    for c in range(8):
        b, q = divmod(c, 4)
        out[b, q * QT:(q + 1) * QT] = np.asarray(res.results[c]["out"], np.float32)
    return out
```
